# Optimizing a Trainium2 kernel written in Bass

```python
import jax, jax.numpy as jnp
from jax import lax
import numpy as np

D_MODEL = 2048
BATCH = 2
SEQ = 16384
DEPTH = 1

MIX_WIDTH = D_MODEL
POOL_WIDTH = MIX_WIDTH // 2
POOL_WINDOWS = (2, 4, 8, 16)
POOL_GROUP = POOL_WIDTH // len(POOL_WINDOWS)
ATTN_WIDTH = MIX_WIDTH - POOL_WIDTH
HEAD_DIM = 128
N_HEADS = ATTN_WIDTH // HEAD_DIM
DILATED_PATTERNS = ((128, 1), (512, 4), (2048, 16))
MAX_WINDOW = max(w for w, _ in DILATED_PATTERNS)
Q_BLOCK = 128
IN_WIDTH = POOL_WIDTH + 3 * ATTN_WIDTH
D_FF = 4 * D_MODEL
NORM_EPS = 1e-6

kernel_name = "hybrid_pool_dilated_attn_block"


def rmsnorm(x, g):
    xf = x.astype(jnp.float32)
    y = xf * lax.rsqrt(jnp.mean(xf * xf, axis=-1, keepdims=True) + NORM_EPS)
    return (y * g.astype(jnp.float32)).astype(x.dtype)


def alibi_slopes():
    return 2.0 ** (-8.0 * jnp.arange(1, N_HEADS + 1, dtype=jnp.float32) / N_HEADS)


def pool_mixer(u, pool_w, pool_scale):
    B, S, _ = u.shape
    uf = u.astype(jnp.float32)
    cs = jnp.cumsum(uf, axis=1)
    pos = jnp.arange(1, S + 1, dtype=jnp.float32)[None, :, None]
    diffs = []
    for g, w in enumerate(POOL_WINDOWS):
        c = cs[..., g * POOL_GROUP:(g + 1) * POOL_GROUP]
        trailing = c - jnp.pad(c, ((0, 0), (w, 0), (0, 0)))[:, :S]
        mean = trailing / jnp.minimum(pos, float(w))
        diffs.append(mean - uf[..., g * POOL_GROUP:(g + 1) * POOL_GROUP])
    d = jnp.stack(diffs, axis=2).astype(u.dtype)
    y = jnp.einsum('bsgc,gcf->bsgf', d, pool_w).reshape(B, S, POOL_WIDTH)
    return y * pool_scale


def dilated_attention(q, k, v, slopes):
    B, S, H, Dh = q.shape
    scale = Dh ** -0.5
    kp = jnp.pad(k, ((0, 0), (MAX_WINDOW, 0), (0, 0), (0, 0)))
    vp = jnp.pad(v, ((0, 0), (MAX_WINDOW, 0), (0, 0), (0, 0)))
    qi = jnp.arange(Q_BLOCK)

    def block(b0):
        qb = lax.dynamic_slice_in_dim(q, b0, Q_BLOCK, axis=1).astype(jnp.float32) * scale
        kwin = lax.dynamic_slice_in_dim(kp, b0, Q_BLOCK + MAX_WINDOW, axis=1)
        vwin = lax.dynamic_slice_in_dim(vp, b0, Q_BLOCK + MAX_WINDOW, axis=1)
        ms, dens, nums = [], [], []
        for window, dil in DILATED_PATTERNS:
            dist = jnp.arange(window // dil + 1) * dil
            idx = qi[:, None] + MAX_WINDOW - dist[None, :]
            valid = (b0 + qi)[:, None] >= dist[None, :]
            kg = jnp.take(kwin, idx, axis=1).astype(jnp.float32)
            vg = jnp.take(vwin, idx, axis=1).astype(jnp.float32)
            s = jnp.einsum('bqhd,bqjhd->bqhj', qb, kg)
            s = s - slopes[:, None] * dist.astype(jnp.float32)[None, :]
            s = jnp.where(valid[None, :, None, :], s, -jnp.inf)
            m = jnp.max(s, axis=-1)
            p = jnp.exp(s - m[..., None])
            ms.append(m)
            dens.append(jnp.sum(p, axis=-1))
            nums.append(jnp.einsum('bqhj,bqjhd->bqhd', p, vg))
        m_all = jnp.stack(ms)
        w = jnp.exp(m_all - jnp.max(m_all, axis=0))
        den = jnp.sum(w * jnp.stack(dens), axis=0)
        num = jnp.sum(w[..., None] * jnp.stack(nums), axis=0)
        return num / den[..., None]

    starts = jnp.arange(S // Q_BLOCK) * Q_BLOCK
    out = lax.map(block, starts)
    return out.transpose(1, 0, 2, 3, 4).reshape(B, S, H * Dh).astype(q.dtype)


def setup_inputs(seed: int = 0) -> dict:
    key = jax.random.key(seed)
    ks = jax.random.split(key, 12)
    f32 = jnp.float32
    x = jax.random.normal(ks[0], (BATCH, SEQ, D_MODEL), f32)
    norm_mix_g = 1.0 + 0.05 * jax.random.normal(ks[1], (DEPTH, D_MODEL), f32)
    w_in = jax.random.normal(ks[2], (DEPTH, D_MODEL, IN_WIDTH), f32) * D_MODEL ** -0.5
    pool_w = jax.random.normal(ks[3], (DEPTH, len(POOL_WINDOWS), POOL_GROUP, POOL_GROUP), f32) * POOL_GROUP ** -0.5
    pool_scale = 1.0 + 0.1 * jax.random.normal(ks[4], (DEPTH, POOL_WIDTH), f32)
    pool_out_norm_g = 1.0 + 0.05 * jax.random.normal(ks[5], (DEPTH, POOL_WIDTH), f32)
    attn_out_norm_g = 1.0 + 0.05 * jax.random.normal(ks[6], (DEPTH, ATTN_WIDTH), f32)
    w_out = jax.random.normal(ks[7], (DEPTH, MIX_WIDTH, D_MODEL), f32) * MIX_WIDTH ** -0.5
    norm_mlp_g = 1.0 + 0.05 * jax.random.normal(ks[8], (DEPTH, D_MODEL), f32)
    w_up = jax.random.normal(ks[9], (DEPTH, D_MODEL, D_FF), f32) * D_MODEL ** -0.5
    w_down = jax.random.normal(ks[10], (DEPTH, D_FF, D_MODEL), f32) * D_FF ** -0.5
    norm_final_g = 1.0 + 0.05 * jax.random.normal(ks[11], (D_MODEL,), f32)
    return {"x": x, "norm_mix_g": norm_mix_g, "w_in": w_in, "pool_w": pool_w,
            "pool_scale": pool_scale, "pool_out_norm_g": pool_out_norm_g,
            "attn_out_norm_g": attn_out_norm_g, "w_out": w_out, "norm_mlp_g": norm_mlp_g,
            "w_up": w_up, "w_down": w_down, "norm_final_g": norm_final_g}


def reference(x, norm_mix_g, w_in, pool_w, pool_scale, pool_out_norm_g, attn_out_norm_g,
              w_out, norm_mlp_g, w_up, w_down, norm_final_g):
    B, S, _ = x.shape
    slopes = alibi_slopes()
    for l in range(DEPTH):
        h = rmsnorm(x, norm_mix_g[l])
        proj = h @ w_in[l]
        u = proj[..., :POOL_WIDTH]
        qkv = proj[..., POOL_WIDTH:].reshape(B, S, 3, N_HEADS, HEAD_DIM)
        y_pool = rmsnorm(pool_mixer(u, pool_w[l], pool_scale[l]), pool_out_norm_g[l])
        y_attn = rmsnorm(dilated_attention(qkv[:, :, 0], qkv[:, :, 1], qkv[:, :, 2], slopes),
                         attn_out_norm_g[l])
        x = x + jnp.concatenate([y_pool, y_attn], axis=-1) @ w_out[l]
        h = rmsnorm(x, norm_mlp_g[l])
        x = x + jnp.square(jax.nn.relu(h @ w_up[l])) @ w_down[l]
    return rmsnorm(x, norm_final_g)
```

```python
import numpy as np
from contextlib import ExitStack
import concourse.bass as bass
import concourse.mybir as mybir
from concourse.bass_utils import run_bass_kernel_spmd

F32 = mybir.dt.float32
BF16 = mybir.dt.bfloat16
AF = mybir.ActivationFunctionType
ALU = mybir.AluOpType

D = 2048
DFF = 8192
NOWN = 4096
HALO = 2048
NTOK = NOWN + HALO
T = 512
NT_ALL = NTOK // T
NT_H = HALO // T
NT_OWN = NOWN // T
EPS = 1e-6
QSCALE = 128 ** -0.5
PATS = (1, 4, 16)
SB_BASE = 20480
SB_LIMIT = 229376


class Res:
    __slots__ = ("name", "w", "rd")

    def __init__(self, name=""):
        self.name = name
        self.w = {}
        self.rd = {}


class Op:
    __slots__ = ("eng", "fn", "kind", "sem", "inc", "waits", "idx", "cnt")


class Prog:
    ENGS = ("pe", "act", "dve", "pool", "sp")

    def __init__(self, nc, stack):
        self.nc = nc
        self.stack = stack
        self.streams = {e: [] for e in self.ENGS}
        self.esem = {e: stack.enter_context(nc.semaphore("es_" + e)) for e in self.ENGS}
        self.dsem = {}
        self.dma_tot = {}
        self.last_c = {}

    def op(self, eng, fn, r=(), w=(), kind="c", sem=None):
        o = Op()
        o.eng = eng
        o.fn = fn
        o.kind = kind
        o.sem = sem
        o.inc = False
        o.waits = {}
        o.cnt = 0
        deps = []
        for x in r:
            for wr in x.w.values():
                deps.append((wr, True))
        for x in w:
            for wr in x.w.values():
                deps.append((wr, False))
            for rd in x.rd.values():
                deps.append((rd, False))
        for d, raw in deps:
            if d is o:
                continue
            if d.kind == "d":
                key = ("d", d.sem)
                o.waits[key] = max(o.waits.get(key, 0), self.dma_tot[d.sem])
            else:
                if kind == "c" and d.eng == eng:
                    if eng == "pe" or not raw:
                        continue
                d.inc = True
                key = ("c", d.eng)
                prev = o.waits.get(key)
                if prev is None or d.idx > prev.idx:
                    o.waits[key] = d
        o.idx = len(self.streams[eng])
        self.streams[eng].append(o)
        for x in r:
            x.rd[(eng, kind, sem)] = o
        for x in w:
            x.w[(eng, kind, sem)] = o
            x.rd = {}
        if kind == "d":
            self.dma_tot[sem] += 16
        else:
            self.last_c[eng] = o
        return o

    def dma(self, q, out, in_, r=(), w=(), sem="g"):
        if sem not in self.dsem:
            self.dsem[sem] = self.stack.enter_context(self.nc.semaphore("ds_" + sem))
            self.dma_tot[sem] = 0
        return self.op(q, lambda e: e.dma_start(out=out, in_=in_), r, w, kind="d", sem=sem)

    def barrier(self, engs=None):
        for e in (engs or self.ENGS):
            o = Op()
            o.eng = e
            o.fn = None
            o.kind = "c"
            o.sem = None
            o.inc = False
            o.cnt = 0
            o.waits = {}
            for g, l in self.last_c.items():
                if not (g == e and e == "pe"):
                    l.inc = True
                    o.waits[("c", g)] = l
            for s, t in self.dma_tot.items():
                if t:
                    o.waits[("d", s)] = t
            o.idx = len(self.streams[e])
            self.streams[e].append(o)

    def emit(self):
        for eng, ops in self.streams.items():
            c = 0
            for o in ops:
                if o.kind == "c" and o.inc:
                    c += 1
                o.cnt = c
        P = self

        def run(name, e):
            waited = {}
            for o in P.streams[name]:
                for key, v in o.waits.items():
                    if key[0] == "d":
                        sem = P.dsem[key[1]]
                        val = v
                    else:
                        sem = P.esem[key[1]]
                        val = v.cnt
                    if waited.get(key, 0) < val:
                        e.wait_ge(sem, val)
                        waited[key] = val
                if o.fn is None:
                    continue
                ins = o.fn(e)
                if o.kind == "d":
                    ins.then_inc(P.dsem[o.sem], 16)
                elif o.inc:
                    ins.then_inc(P.esem[name], 1)

        with self.nc.Block() as block:
            block.tensor(lambda e: run("pe", e))
            block.scalar(lambda e: run("act", e))
            block.vector(lambda e: run("dve", e))
            block.gpsimd(lambda e: run("pool", e))
            block.sync(lambda e: run("sp", e))


def build(stop_after=99, dbg=False, skip0=False):
    nc = bass.Bass("TRN2", target_bir_lowering=False)
    dk = "ExternalOutput" if dbg else "Internal"

    def din(name, shape, dt=F32):
        return nc.dram_tensor(name, shape, dt, kind="ExternalInput").ap()

    xh = din("xh", [NTOK, D])
    w_in = din("w_in", [D, 4096])
    pool_w = din("pool_w", [4, 256, 256])
    if not skip0:
        w_out = din("w_out", [D, D])
        w_up = din("w_up", [D, DFF])
        w_down = din("w_down", [DFF, D])
    vecs = din("vecs", [128, 64])
    gfin_d = din("gfin", [128, D])
    tab_d = din("tab", [128, 8, 3, 2, 128])
    tabf_d = din("tabf", [128, 8, 3, 128])
    invc_d = din("invc", [128, 4, T])
    idf_d = din("idf", [128, 128])
    out = nc.dram_tensor("out", [NOWN, D], F32, kind="ExternalOutput").ap()

    wup_s = nc.dram_tensor("wup_s", [16, 128, 16, 512], BF16).ap()
    wdn_s = nc.dram_tensor("wdn_s", [4, 8, 128, 8, 512], BF16).ap()
    wout_s = nc.dram_tensor("wout_s", [4, 128, 16, 512], BF16).ap()
    qT_s = nc.dram_tensor("qT_s", [8, 128, NOWN], BF16, kind=dk).ap()
    kT_s = nc.dram_tensor("kT_s", [8, 128, NTOK], BF16, kind=dk).ap()
    v_s = nc.dram_tensor("v_s", [NTOK, 1024], BF16, kind=dk).ap()
    aT_s = nc.dram_tensor("aT_s", [8, 128, NOWN], BF16, kind=dk).ap()
    bT_s = nc.dram_tensor("bT_s", [8, 128, NOWN], BF16, kind=dk).ap()

    with ExitStack() as stack:
        P = Prog(nc, stack)
        ps = [stack.enter_context(nc.psum_tensor("ps%d" % i, [128, 512], F32)) for i in range(8)]
        RP = [Res("ps%d" % i) for i in range(8)]
        bank = [0]

        def nb():
            b = bank[0]
            bank[0] = (b + 1) % 8
            return b

        off = [SB_BASE]
        ntens = [0]

        def sb(shape, dt, name="t"):
            nbytes = int(np.prod(shape[1:])) * (4 if dt == F32 else 2)
            ntens[0] += 1
            t = nc.alloc_sbuf_tensor_at("%s_%d" % (name, ntens[0]), shape, dt, offset=off[0])
            off[0] += (nbytes + 63) // 64 * 64
            assert off[0] <= SB_LIMIT, (name, off[0])
            return t

        idf_s = sb([128, 128], F32, "idf")
        idb = sb([128, 128], BF16, "idb")
        onesb = sb([128, 128], BF16, "ones")
        vec_s = sb([128, 64], F32, "vecs")
        ssq = sb([128, 64], F32, "ssq")
        junk = sb([128, D], BF16, "junk")
        R_id = Res()
        R_vec = Res()
        R_junk = Res()
        R_ssq = [Res() for _ in range(16)]
        R_ssqc = [Res() for _ in range(64)]
        ssq_ctr = [0]
        P.dma("sp", idf_s[:], idf_d, w=[R_id], sem="c0")
        P.dma("sp", vec_s[:], vecs, w=[R_vec], sem="c0")
        P.op("dve", lambda e: e.tensor_copy(out=idb[:], in_=idf_s[:]), r=[R_id], w=[R_id])
        P.op("dve", lambda e: e.memset(onesb[:], 1.0), w=[R_id])
        base_persist = off[0]

        def rstd_chain(grp, ncol, scale_div):
            c0 = grp * 4
            R = R_ssq[grp]
            P.op("dve", lambda e: e.tensor_scalar(out=ssq[:, c0:c0 + ncol], in0=ssq[:, c0:c0 + ncol], scalar1=1.0 / scale_div,
                                                  scalar2=EPS, op0=ALU.mult, op1=ALU.add), r=[R], w=[R])
            P.op("act", lambda e: e.activation(out=ssq[:, c0:c0 + ncol], in_=ssq[:, c0:c0 + ncol], func=AF.Sqrt), r=[R], w=[R])
            P.op("dve", lambda e: e.reciprocal(out=ssq[:, c0:c0 + ncol], in_=ssq[:, c0:c0 + ncol]), r=[R], w=[R])

        def rstd_chain_col(col, scale_div):
            R = R_ssqc[col]
            P.op("dve", lambda e: e.tensor_scalar(out=ssq[:, col:col + 1], in0=ssq[:, col:col + 1], scalar1=1.0 / scale_div,
                                                  scalar2=EPS, op0=ALU.mult, op1=ALU.add), r=[R], w=[R])
            P.op("act", lambda e: e.activation(out=ssq[:, col:col + 1], in_=ssq[:, col:col + 1], func=AF.Sqrt), r=[R], w=[R])
            P.op("dve", lambda e: e.reciprocal(out=ssq[:, col:col + 1], in_=ssq[:, col:col + 1]), r=[R], w=[R])

        def new_ssq_grp():
            g = ssq_ctr[0] % 16
            ssq_ctr[0] += 1
            c0 = g * 4
            P.op("dve", lambda e: e.memset(ssq[:, c0:c0 + 4], 0.0), w=[R_ssq[g]])
            return g

        evac_tog = [0]

        def evac_copy(out_ap, in_ap, r, w, scale=None):
            evac_tog[0] ^= 1
            if evac_tog[0]:
                if scale is None:
                    P.op("act", lambda e: e.activation(out=out_ap, in_=in_ap, func=AF.Copy), r=r, w=w)
                else:
                    P.op("act", lambda e: e.activation(out=out_ap, in_=in_ap, func=AF.Copy, scale=scale), r=r, w=w)
            else:
                if scale is None:
                    P.op("dve", lambda e: e.tensor_copy(out=out_ap, in_=in_ap), r=r, w=w)
                else:
                    P.op("dve", lambda e: e.tensor_scalar(out=out_ap, in0=in_ap, scalar1=scale, scalar2=None, op0=ALU.mult), r=r, w=w)

        off[0] = base_persist
        NST = 3
        stg_f = [sb([128, D], F32, "stgf") for _ in range(NST)]
        stg_b = [sb([128, D], BF16, "stgb") for _ in range(NST)]
        R_sf = [Res() for _ in range(NST)]
        R_sb = [Res() for _ in range(NST)]
        R_wup, R_wdn, R_wout = Res(), Res(), Res()
        jobs = []
        for kc in range(0 if skip0 else 16):
            for qd in range(4):
                jobs.append((w_up[kc * 128:(kc + 1) * 128, qd * 2048:(qd + 1) * 2048],
                             wup_s[qd * 4:(qd + 1) * 4, :, kc, :].rearrange("i p c -> p i c"), 16 + kc, R_wup))
        for f in range(0 if skip0 else 64):
            jobs.append((w_down[f * 128:(f + 1) * 128, :], wdn_s[:, f // 8, :, f % 8, :].rearrange("cb p c -> p cb c"), None, R_wdn))
        for kc in range(0 if skip0 else 16):
            jobs.append((w_out[kc * 128:(kc + 1) * 128, :], wout_s[:, :, kc, :].rearrange("cb p c -> p cb c"), 32 + kc, R_wout))
        cast_engs = ("dve", "act", "pool")

        def p0_load(i):
            src, dst, gcol, Rw = jobs[i]
            P.dma("sp", stg_f[i % NST][:], src, w=[R_sf[i % NST]], sem="sf%d" % (i % NST))

        def p0_cast(i):
            src, dst, gcol, Rw = jobs[i]
            k = i % NST
            eng = cast_engs[i % 3]
            o_ap = stg_b[k][:]
            i_ap = stg_f[k][:]
            if gcol is not None:
                sc = vec_s[:, gcol:gcol + 1]
                if eng == "act":
                    P.op("act", lambda e: e.activation(out=o_ap, in_=i_ap, func=AF.Copy, scale=sc), r=[R_sf[k], R_vec], w=[R_sb[k]])
                else:
                    P.op(eng, lambda e: e.tensor_scalar(out=o_ap, in0=i_ap, scalar1=sc, scalar2=None, op0=ALU.mult), r=[R_sf[k], R_vec], w=[R_sb[k]])
            else:
                if eng == "act":
                    P.op("act", lambda e: e.activation(out=o_ap, in_=i_ap, func=AF.Copy), r=[R_sf[k]], w=[R_sb[k]])
                else:
                    P.op(eng, lambda e: e.tensor_copy(out=o_ap, in_=i_ap), r=[R_sf[k]], w=[R_sb[k]])
            P.dma("sp", dst, stg_b[k][:].rearrange("p (i c) -> p i c", c=512), r=[R_sb[k]], w=[Rw], sem="sb%d" % k)

        if skip0:
            jobs = []
        for i in range(min(NST, len(jobs))):
            p0_load(i)
        for i in range(len(jobs)):
            p0_cast(i)
            if i + NST < len(jobs):
                p0_load(i + NST)
        P.barrier()
        if stop_after <= 0:
            P.barrier(["sp"])
            P.emit()
            return nc

        off[0] = base_persist
        Wres = sb([128, 16, 2048], BF16, "wres")
        R_W = Res()
        wst = [sb([128, 2048], F32, "wst") for _ in range(2)]
        R_wst = [Res(), Res()]
        NXR = 3
        xt = [sb([128, D], F32, "xt") for _ in range(NXR)]
        R_xt = [Res() for _ in range(NXR)]
        NHR = 3
        hb = [sb([128, D], BF16, "hb") for _ in range(NHR)]
        R_hb = [Res() for _ in range(NHR)]
        hT = [sb([128, 16, T], BF16, "hT")]
        R_hT = [Res(), Res()]
        base_p1 = off[0]
        hT.append(sb([128, 16, T], BF16, "hT"))

        def load_wres(col0, qscale_cols=None):
            for kc in range(16):
                k = kc % 2
                P.dma("sp", wst[k][:], w_in[kc * 128:(kc + 1) * 128, col0:col0 + 2048], w=[R_wst[k]], sem="wst%d" % k)
                sc = vec_s[:, kc:kc + 1]
                if qscale_cols is None:
                    P.op("dve" if kc % 2 else "pool", lambda e, kc=kc, k=k, sc=sc: e.tensor_scalar(
                        out=Wres[:, kc, :], in0=wst[k][:], scalar1=sc, scalar2=None, op0=ALU.mult), r=[R_wst[k], R_vec], w=[R_W])
                else:
                    a, b = qscale_cols
                    P.op("dve", lambda e, kc=kc, k=k, sc=sc: e.tensor_scalar(
                        out=Wres[:, kc, 0:a], in0=wst[k][:, 0:a], scalar1=sc, scalar2=None, op0=ALU.mult), r=[R_wst[k], R_vec], w=[R_W])
                    P.op("pool", lambda e, kc=kc, k=k, sc=sc: e.tensor_scalar(
                        out=Wres[:, kc, a:b], in0=wst[k][:, a:b], scalar1=sc, scalar2=QSCALE, op0=ALU.mult, op1=ALU.mult), r=[R_wst[k], R_vec], w=[R_W])

        hctr = [0]
        xjobs = []
        xiss = [0]
        xcons = [0]

        def xload_ensure(n):
            while xiss[0] < min(n, len(xjobs)):
                i = xiss[0]
                k = i % NXR
                r0 = xjobs[i]
                P.dma("sp", xt[k][:], xh[r0:r0 + 128, :], w=[R_xt[k]], sem="xt%d" % k)
                xiss[0] += 1

        def front(j, hTi):
            g = new_ssq_grp()
            for s in range(4):
                i = xcons[0]
                xcons[0] += 1
                assert xjobs[i] == j * T + s * 128
                xload_ensure(i + NXR)
                k = i % NXR
                col = g * 4 + s
                Rc = R_ssqc[col]
                P.op("act", lambda e, k=k, col=col: e.activation(out=junk[:], in_=xt[k][:], func=AF.Square, accum_out=ssq[:, col:col + 1]),
                     r=[R_xt[k], R_ssq[g]], w=[R_junk, Rc])
                rstd_chain_col(col, D)
                hk = hctr[0] % NHR
                hctr[0] += 1
                P.op("dve" if s % 2 == 0 else "pool", lambda e, k=k, col=col, hk=hk: e.tensor_scalar(
                    out=hb[hk][:], in0=xt[k][:], scalar1=ssq[:, col:col + 1], scalar2=None, op0=ALU.mult),
                    r=[R_xt[k], Rc], w=[R_hb[hk]])
                for half in range(2):
                    b = nb()
                    psb = ps[b][:].bitcast(BF16)
                    for kk in range(8):
                        kc = half * 8 + kk
                        P.op("pe", lambda e, psb=psb, kk=kk, kc=kc, hk=hk: e.transpose(
                            out=psb[:, kk * 128:(kk + 1) * 128], in_=hb[hk][:, kc * 128:(kc + 1) * 128], identity=idb[:]),
                            r=[R_hb[hk], R_id], w=[RP[b]])
                    evac_copy(hT[hTi][:, half * 8:(half + 1) * 8, s * 128:(s + 1) * 128],
                              psb[:, 0:1024].rearrange("p (k t) -> p k t", t=128), r=[RP[b]], w=[R_hT[hTi]])

        kst = [sb([128, 8, T], BF16, "kst") for _ in range(2)]
        vst = [sb([128, 4, 1024], BF16, "vst") for _ in range(2)]
        R_kst = [Res(), Res()]
        R_vst = [Res(), Res()]
        R_kT, R_v, R_qT, R_aT, R_bT = Res(), Res(), Res(), Res(), Res()
        load_wres(2048)
        xjobs.extend(j * T + s * 128 for j in range(NT_ALL) for s in range(4))
        for j in range(NT_ALL):
            hi = j % 2
            front(j, hi)
            k2 = j % 2
            for hd in range(8):
                b = nb()
                for kc in range(16):
                    P.op("pe", lambda e, b=b, kc=kc, hd=hd, hi=hi: e.matmul(ps[b][:], lhsT=Wres[:, kc, hd * 128:(hd + 1) * 128], rhs=hT[hi][:, kc, :],
                                                                         start=(kc == 0), stop=(kc == 15)), r=[R_W, R_hT[hi]], w=[RP[b]])
                evac_copy(kst[k2][:, hd, :], ps[b][:], r=[RP[b]], w=[R_kst[k2]])
            P.dma("sp", kT_s[:, :, j * T:(j + 1) * T].rearrange("h p t -> p h t"), kst[k2][:], r=[R_kst[k2]], w=[R_kT], sem="kst%d" % k2)
            for s in range(4):
                for half in range(2):
                    b = nb()
                    for kc in range(16):
                        P.op("pe", lambda e, b=b, kc=kc, s=s, half=half, hi=hi: e.matmul(
                            ps[b][:], lhsT=hT[hi][:, kc, s * 128:(s + 1) * 128], rhs=Wres[:, kc, 1024 + half * 512:1024 + (half + 1) * 512],
                            start=(kc == 0), stop=(kc == 15)), r=[R_W, R_hT[hi]], w=[RP[b]])
                    evac_copy(vst[k2][:, s, half * 512:(half + 1) * 512], ps[b][:], r=[RP[b]], w=[R_vst[k2]])
            P.dma("sp", v_s[j * T:(j + 1) * T, :].rearrange("(s p) c -> p s c", p=128), vst[k2][:], r=[R_vst[k2]], w=[R_v], sem="vst%d" % k2)
        P.barrier()
        if stop_after <= 1:
            P.barrier(["sp"])
            P.emit()
            return nc

        off[0] = base_p1
        U = sb([128, 8, 528], F32, "U")
        SA = sb([128, 2, 528], F32, "SA")
        SBb = sb([128, 2, 528], F32, "SB")
        dT = sb([128, 8, T], BF16, "dT")
        qst = sb([128, 8, T], BF16, "qst")
        ast = sb([128, 8, T], BF16, "ast")
        invc = sb([128, 4, T], F32, "invc")
        pwf = wst[0][:].rearrange("p (g c f) -> p g c f", g=4, c=2)
        pwb = sb([128, 4, 2, 256], BF16, "pwb")
        R_U, R_SA, R_SB, R_dT, R_qst, R_ast, R_invc, R_pw = Res(), Res(), Res(), Res(), Res(), Res(), Res(), Res()
        P.dma("sp", invc[:], invc_d, w=[R_invc], sem="c1")
        P.dma("sp", pwf, pool_w.rearrange("g (cc p) f -> p g cc f", p=128), w=[R_wst[0]], sem="wst0")
        P.op("dve", lambda e: e.tensor_copy(out=pwb[:], in_=pwf), r=[R_wst[0]], w=[R_pw])
        load_wres(0, qscale_cols=(1024, 2048))

        def y_mm(j):
            for g in range(4):
                for fc in range(2):
                    b = nb()
                    for cc in range(2):
                        P.op("pe", lambda e, b=b, g=g, fc=fc, cc=cc: e.matmul(ps[b][:], lhsT=pwb[:, g, cc, fc * 128:(fc + 1) * 128], rhs=dT[:, 2 * g + cc, :],
                                                                             start=(cc == 0), stop=(cc == 1)), r=[R_pw, R_dT], w=[RP[b]])
                    ch = 2 * g + fc
                    P.op("act", lambda e, b=b, ch=ch: e.activation(out=ast[:, ch, :], in_=ps[b][:], func=AF.Copy, scale=vec_s[:, 48 + ch:49 + ch]),
                         r=[RP[b], R_vec], w=[R_ast])
            P.dma("sp", aT_s[:, :, (j - NT_H) * T:(j - NT_H + 1) * T].rearrange("c p t -> p c t"), ast[:], r=[R_ast], w=[R_aT], sem="ast")

        xjobs.extend(j * T + s * 128 for j in range(NT_H - 1, NT_ALL) for s in range(4))
        for j in range(NT_H - 1, NT_ALL):
            front(j, 0)
            if j >= NT_H:
                for hd in range(8):
                    b = nb()
                    for kc in range(16):
                        P.op("pe", lambda e, b=b, kc=kc, hd=hd: e.matmul(ps[b][:], lhsT=Wres[:, kc, 1024 + hd * 128:1024 + (hd + 1) * 128], rhs=hT[0][:, kc, :],
                                                                     start=(kc == 0), stop=(kc == 15)), r=[R_W, R_hT[0]], w=[RP[b]])
                    evac_copy(qst[:, hd, :], ps[b][:], r=[RP[b]], w=[R_qst])
                P.dma("sp", qT_s[:, :, (j - NT_H) * T:(j - NT_H + 1) * T].rearrange("h p t -> p h t"), qst[:], r=[R_qst], w=[R_qT], sem="qst")
            if j > NT_H:
                y_mm(j - 1)
            for c in range(8):
                b = nb()
                for kc in range(16):
                    P.op("pe", lambda e, b=b, kc=kc, c=c: e.matmul(ps[b][:], lhsT=Wres[:, kc, c * 128:(c + 1) * 128], rhs=hT[0][:, kc, :],
                                                               start=(kc == 0), stop=(kc == 15)), r=[R_W, R_hT[0]], w=[RP[b]])
                evac_copy(U[:, c, 16:528], ps[b][:], r=[RP[b]], w=[R_U])
            if j >= NT_H:
                for g in range(4):
                    w_ = 2 ** (g + 1)
                    Ug = U[:, 2 * g:2 * g + 2, :]
                    P.op("pool", lambda e, Ug=Ug: e.tensor_tensor(out=SA[:, :, 2:528], in0=Ug[:, :, 2:528], in1=Ug[:, :, 1:527], op=ALU.add), r=[R_U], w=[R_SA])
                    Tr, R_T = SA, R_SA
                    if g >= 1:
                        P.op("pool", lambda e: e.tensor_tensor(out=SBb[:, :, 4:528], in0=SA[:, :, 4:528], in1=SA[:, :, 2:526], op=ALU.add), r=[R_SA], w=[R_SB])
                        Tr, R_T = SBb, R_SB
                    if g >= 2:
                        P.op("pool", lambda e: e.tensor_tensor(out=SA[:, :, 8:528], in0=SBb[:, :, 8:528], in1=SBb[:, :, 4:524], op=ALU.add), r=[R_SB], w=[R_SA])
                        Tr, R_T = SA, R_SA
                    if g >= 3:
                        P.op("pool", lambda e: e.tensor_tensor(out=SBb[:, :, 16:528], in0=SA[:, :, 16:528], in1=SA[:, :, 8:520], op=ALU.add), r=[R_SA], w=[R_SB])
                        Tr, R_T = SBb, R_SB
                    if j == NT_H:
                        for cc in range(2):
                            P.op("pool", lambda e, Tr=Tr, cc=cc, g=g: e.tensor_tensor(out=Tr[:, cc, 16:528], in0=Tr[:, cc, 16:528], in1=invc[:, g, :], op=ALU.mult),
                                 r=[R_T, R_invc], w=[R_T])
                        P.op("pool", lambda e, Tr=Tr, Ug=Ug, g=g: e.tensor_tensor(out=dT[:, 2 * g:2 * g + 2, :], in0=Tr[:, :, 16:528], in1=Ug[:, :, 16:528], op=ALU.subtract),
                             r=[R_T, R_U], w=[R_dT])
                    else:
                        P.op("pool", lambda e, Tr=Tr, w_=w_: e.tensor_scalar(out=Tr[:, :, 16:528], in0=Tr[:, :, 16:528], scalar1=1.0 / w_, scalar2=None, op0=ALU.mult),
                             r=[R_T], w=[R_T])
                        P.op("pool", lambda e, Tr=Tr, Ug=Ug, g=g: e.tensor_tensor(out=dT[:, 2 * g:2 * g + 2, :], in0=Tr[:, :, 16:528], in1=Ug[:, :, 16:528], op=ALU.subtract),
                             r=[R_T, R_U], w=[R_dT])
            P.op("pool", lambda e: e.tensor_copy(out=U[:, :, 0:16], in_=U[:, :, 512:528]), r=[R_U], w=[R_U])
        y_mm(NT_ALL - 1)
        P.barrier()
        if stop_after <= 2:
            P.barrier(["sp"])
            P.emit()
            return nc

        off[0] = base_persist
        QT = [sb([128, NOWN], BF16, "QT") for _ in range(2)]
        KT = [sb([128, NTOK], BF16, "KT") for _ in range(2)]
        V3 = [sb([128, 3, 48, 128], BF16, "V3") for _ in range(2)]
        tabs = [sb([128, 3, 2, 128], F32, "tab") for _ in range(2)]
        tabf = [sb([128, 3, 128], F32, "tabf") for _ in range(2)]
        ND = sb([128, 2, NOWN], F32, "ND")
        BTs = [sb([128, NOWN], BF16, "BTs") for _ in range(2)]
        NSR = 3
        Sb = [sb([128, 512], F32, "Sb") for _ in range(NSR)]
        PT = [sb([128, 512], BF16, "PT") for _ in range(NSR)]
        R_QT, R_KT, R_V3, R_tab = [Res(), Res()], [Res(), Res()], [Res(), Res()], [Res(), Res()]
        R_ND = Res()
        R_BTs = [Res(), Res()]
        R_Sb = [Res() for _ in range(NSR)]
        R_PT = [Res() for _ in range(NSR)]

        def p2_load(hd):
            pr = hd % 2
            P.dma("sp", QT[pr][:], qT_s[hd], r=[R_qT], w=[R_QT[pr]], sem="QT%d" % pr)
            P.dma("sp", KT[pr][:], kT_s[hd], r=[R_kT], w=[R_KT[pr]], sem="KT%d" % pr)
            P.dma("sp", tabs[pr][:], tab_d[:, hd], w=[R_tab[pr]], sem="tab%d" % pr)
            P.dma("sp", tabf[pr][:], tabf_d[:, hd], w=[R_tab[pr]], sem="tab%d" % pr)
            for p, d in enumerate(PATS):
                M = 48 // d
                vv = v_s.rearrange("(m i r) c -> r m i c", r=d, i=128)
                if d == 16:
                    for r16 in range(16):
                        src = vv[r16, :, :, hd * 128:(hd + 1) * 128].rearrange("m i c -> i m c")
                        dst = V3[pr][:, p, r16 * 3:(r16 + 1) * 3, :]
                        P.dma("sp", dst, src, r=[R_v], w=[R_V3[pr]], sem="V3%d" % pr)
                    continue
                for q4 in range(4):
                    if d == 1:
                        src = vv[0, q4 * 12:(q4 + 1) * 12, :, hd * 128:(hd + 1) * 128].rearrange("m i c -> i m c")
                        dst = V3[pr][:, p, q4 * 12:(q4 + 1) * 12, :]
                    elif d == 4:
                        src = vv[q4, :, :, hd * 128:(hd + 1) * 128].rearrange("m i c -> i m c")
                        dst = V3[pr][:, p, q4 * 12:(q4 + 1) * 12, :]
                    else:
                        src = vv[q4 * 4:(q4 + 1) * 4, :, :, hd * 128:(hd + 1) * 128].rearrange("r m i c -> i r m c")
                        dst = V3[pr][:, p, q4 * 12:(q4 + 1) * 12, :].rearrange("i (r m) c -> i r m c", m=3)
                    P.dma("sp", dst, src, r=[R_v], w=[R_V3[pr]], sem="V3%d" % pr)

        sctr = [0]
        p2_load(0)
        for hd in range(8):
            pr = hd % 2
            if hd + 1 < 8:
                p2_load(hd + 1)
            for p, d in enumerate(PATS):
                M = 48 // d
                m0 = (HALO // 128) // d
                nqb = 32 // d
                KTv = KT[pr][:].rearrange("p (m i r) -> p r m i", r=d, i=128)
                QTv = QT[pr][:].rearrange("p (m i r) -> p r m i", r=d, i=128)
                NDv = ND[:].rearrange("p n (m i r) -> p n r m i", r=d, i=128)
                for r_ in range(d):
                    for mm0 in range(0, nqb, 2):
                        sr = sctr[0] % NSR
                        sctr[0] += 1
                        bs = nb()
                        for qb in range(2):
                            m = m0 + mm0 + qb
                            for c in range(2):
                                P.op("pe", lambda e, bs=bs, qb=qb, c=c, m=m, r_=r_, mm0=mm0, KTv=KTv, QTv=QTv: e.matmul(
                                    ps[bs][:, (qb * 2 + c) * 128:(qb * 2 + c + 1) * 128], lhsT=KTv[:, r_, m - 1 + c, :], rhs=QTv[:, r_, mm0 + qb, :],
                                    start=True, stop=True), r=[R_KT[pr], R_QT[pr]], w=[RP[bs]])
                        for qb in range(2):
                            if mm0 + qb == 0:
                                P.op("dve", lambda e, bs=bs, qb=qb, sr=sr, p=p, pr=pr: e.tensor_tensor(
                                    out=Sb[sr][:, qb * 256:qb * 256 + 128], in0=ps[bs][:, qb * 256:qb * 256 + 128], in1=tabf[pr][:, p, :], op=ALU.add),
                                    r=[RP[bs], R_tab[pr]], w=[R_Sb[sr]])
                                P.op("dve", lambda e, bs=bs, qb=qb, sr=sr, p=p, pr=pr: e.tensor_tensor(
                                    out=Sb[sr][:, qb * 256 + 128:qb * 256 + 256], in0=ps[bs][:, qb * 256 + 128:qb * 256 + 256], in1=tabs[pr][:, p, 1, :], op=ALU.add),
                                    r=[RP[bs], R_tab[pr]], w=[R_Sb[sr]])
                            else:
                                P.op("dve", lambda e, bs=bs, qb=qb, sr=sr, p=p, pr=pr: e.tensor_tensor(
                                    out=Sb[sr][:, qb * 256:(qb + 1) * 256], in0=ps[bs][:, qb * 256:(qb + 1) * 256],
                                    in1=tabs[pr][:, p, :, :].rearrange("k c q -> k (c q)"), op=ALU.add),
                                    r=[RP[bs], R_tab[pr]], w=[R_Sb[sr]])
                        P.op("act", lambda e, sr=sr: e.activation(out=PT[sr][:], in_=Sb[sr][:], func=AF.Exp), r=[R_Sb[sr]], w=[R_PT[sr]])
                        bo = nb()
                        for qb in range(2):
                            m = m0 + mm0 + qb
                            for c in range(2):
                                tau = r_ * M + m - 1 + c
                                P.op("pe", lambda e, bo=bo, qb=qb, c=c, tau=tau, sr=sr, p=p, pr=pr: e.matmul(
                                    ps[bo][:, qb * 128:(qb + 1) * 128], lhsT=V3[pr][:, p, tau, :], rhs=PT[sr][:, (qb * 2 + c) * 128:(qb * 2 + c + 1) * 128],
                                    start=(c == 0), stop=(c == 1)), r=[R_V3[pr], R_PT[sr]], w=[RP[bo]])
                            for c in range(2):
                                P.op("pe", lambda e, bo=bo, qb=qb, c=c, sr=sr: e.matmul(
                                    ps[bo][:, 256 + qb * 128:256 + (qb + 1) * 128], lhsT=onesb[:], rhs=PT[sr][:, (qb * 2 + c) * 128:(qb * 2 + c + 1) * 128],
                                    start=(c == 0), stop=(c == 1)), r=[R_id, R_PT[sr]], w=[RP[bo]])
                        for qb in range(2):
                            tgt = NDv[:, :, r_, mm0 + qb, :]
                            src = ps[bo][:].rearrange("p (n q i) -> p n q i", n=2, q=2)[:, :, qb, :]
                            if p == 0:
                                P.op("act", lambda e, tgt=tgt, src=src: e.activation(out=tgt, in_=src, func=AF.Copy), r=[RP[bo]], w=[R_ND])
                            else:
                                P.op("dve", lambda e, tgt=tgt, src=src: e.tensor_tensor(out=tgt, in0=src, in1=tgt, op=ALU.add), r=[RP[bo], R_ND], w=[R_ND])
            for q4 in range(4):
                sl = slice(q4 * 1024, (q4 + 1) * 1024)
                P.op("dve", lambda e, sl=sl: e.reciprocal(out=ND[:, 1, sl], in_=ND[:, 1, sl]), r=[R_ND], w=[R_ND])
                P.op("pool", lambda e, sl=sl, pr=pr: e.tensor_tensor(out=BTs[pr][:, sl], in0=ND[:, 0, sl], in1=ND[:, 1, sl], op=ALU.mult), r=[R_ND], w=[R_BTs[pr]])
            P.dma("sp", bT_s[hd], BTs[pr][:], r=[R_BTs[pr]], w=[R_bT], sem="BTs%d" % pr)
        P.barrier()
        if stop_after <= 3:
            P.barrier(["sp"])
            P.emit()
            return nc

        off[0] = base_persist
        xr = [sb([128, D], F32, "xr") for _ in range(4)]
        R_xr = [Res() for _ in range(4)]
        yT = sb([128, 16, T], BF16, "yT")
        R_yT = Res()
        ysq = [sb([128, T], BF16, "ysq") for _ in range(3)]
        R_ysq = [Res() for _ in range(3)]
        gfin = sb([128, D], F32, "gfin")
        R_gfin = Res()
        h2 = [sb([128, D], BF16, "h2") for _ in range(2)]
        R_h2 = [Res(), Res()]
        h2T = sb([128, 16, T], BF16, "h2T")
        R_h2T = Res()
        aT = sb([128, 64, T], BF16, "aT")
        R_aTc = [Res() for _ in range(64)]
        wring = [sb([128, 16, 512], BF16, "wring") for _ in range(2)]
        R_wr = [Res(), Res()]
        wdr = [sb([128, 8, 512], BF16, "wdr") for _ in range(2)]
        R_wd = [Res(), Res()]
        rpa = sb([128, 8], F32, "rpa")
        R_rpa = Res()
        P.dma("sp", gfin[:], gfin_d, w=[R_gfin], sem="c1")

        wr_list = []
        wd_list = []
        for j in range(NT_OWN):
            for cb in range(4):
                wr_list.append((wout_s[cb], R_wout))
            for i in range(16):
                wr_list.append((wup_s[i], R_wup))
            for cb in range(4):
                for fg in range(8):
                    wd_list.append((wdn_s[cb, fg], R_wdn))
        wr_iss = [0]
        wd_iss = [0]

        def wr_ensure(n):
            while wr_iss[0] < min(n, len(wr_list)):
                i = wr_iss[0]
                src, Rs = wr_list[i]
                P.dma("sp", wring[i % 2][:], src, r=[Rs], w=[R_wr[i % 2]], sem="wr%d" % (i % 2))
                wr_iss[0] += 1

        def wd_ensure(n):
            while wd_iss[0] < min(n, len(wd_list)):
                i = wd_iss[0]
                src, Rs = wd_list[i]
                P.dma("sp", wdr[i % 2][:], src, r=[Rs], w=[R_wd[i % 2]], sem="wd%d" % (i % 2))
                wd_iss[0] += 1

        wr_c = [0]
        wd_c = [0]
        yq = [0]
        h2c = [0]

        def p3_loads(j):
            P.dma("sp", yT[:, 0:8, :], aT_s[:, :, j * T:(j + 1) * T].rearrange("c p t -> p c t"), r=[R_aT], w=[R_yT], sem="yT")
            P.dma("sp", yT[:, 8:16, :], bT_s[:, :, j * T:(j + 1) * T].rearrange("c p t -> p c t"), r=[R_bT], w=[R_yT], sem="yT")
            for s in range(4):
                r0 = HALO + j * T + s * 128
                P.dma("sp", xr[s][:], xh[r0:r0 + 128, :], w=[R_xr[s]], sem="xr%d" % s)

        for j in range(NT_OWN):
            wr_ensure(wr_c[0] + 1)
            p3_loads(j)
            bq = nb()
            P.op("dve", lambda e, bq=bq: e.memset(ps[bq][:, 0:8], 0.0), w=[RP[bq]])
            for kc in range(16):
                yk = yq[0] % 3
                yq[0] += 1
                P.op("pool", lambda e, kc=kc, yk=yk: e.tensor_tensor(out=ysq[yk][:], in0=yT[:, kc, :], in1=yT[:, kc, :], op=ALU.mult), r=[R_yT], w=[R_ysq[yk]])
                for s in range(4):
                    col = (kc // 8) * 4 + s
                    P.op("pe", lambda e, bq=bq, yk=yk, s=s, col=col: e.matmul(ps[bq][:, col:col + 1], lhsT=ysq[yk][:, s * 128:(s + 1) * 128], rhs=onesb[:, 0:1],
                                                                            start=False, stop=False, skip_group_check=True), r=[R_ysq[yk], R_id], w=[RP[bq]])
            P.op("dve", lambda e, bq=bq: e.tensor_scalar(out=rpa[:], in0=ps[bq][:, 0:8], scalar1=1.0 / 1024, scalar2=EPS, op0=ALU.mult, op1=ALU.add), r=[RP[bq]], w=[R_rpa])
            P.op("act", lambda e: e.activation(out=rpa[:], in_=rpa[:], func=AF.Sqrt), r=[R_rpa], w=[R_rpa])
            P.op("dve", lambda e: e.reciprocal(out=rpa[:], in_=rpa[:]), r=[R_rpa], w=[R_rpa])
            for cb in range(4):
                wi = wr_c[0]
                wr_c[0] += 1
                wr_ensure(wi + 2)
                wk = wi % 2
                for s in range(4):
                    bp, ba = nb(), nb()
                    for kc in range(16):
                        b = bp if kc < 8 else ba
                        P.op("pe", lambda e, b=b, kc=kc, s=s, wk=wk: e.matmul(ps[b][:], lhsT=yT[:, kc, s * 128:(s + 1) * 128], rhs=wring[wk][:, kc, :],
                                                                            start=(kc % 8 == 0), stop=(kc % 8 == 7)), r=[R_yT, R_wr[wk]], w=[RP[b]])
                    xs = xr[s][:, cb * 512:(cb + 1) * 512]
                    P.op("dve", lambda e, bp=bp, s=s, xs=xs: e.scalar_tensor_tensor(out=xs, in0=ps[bp][:], scalar=rpa[:, s:s + 1], in1=xs, op0=ALU.mult, op1=ALU.add),
                         r=[RP[bp], R_rpa, R_xr[s]], w=[R_xr[s]])
                    P.op("dve", lambda e, ba=ba, s=s, xs=xs: e.scalar_tensor_tensor(out=xs, in0=ps[ba][:], scalar=rpa[:, 4 + s:5 + s], in1=xs, op0=ALU.mult, op1=ALU.add),
                         r=[RP[ba], R_rpa, R_xr[s]], w=[R_xr[s]])
            g = new_ssq_grp()
            for s in range(4):
                P.op("act", lambda e, s=s, g=g: e.activation(out=junk[:], in_=xr[s][:], func=AF.Square, accum_out=ssq[:, g * 4 + s:g * 4 + s + 1]),
                     r=[R_xr[s]], w=[R_junk, R_ssq[g]])
            rstd_chain(g, 4, D)
            for s in range(4):
                hk = h2c[0] % 2
                h2c[0] += 1
                P.op("dve" if s % 2 == 0 else "pool", lambda e, s=s, g=g, hk=hk: e.tensor_scalar(
                    out=h2[hk][:], in0=xr[s][:], scalar1=ssq[:, g * 4 + s:g * 4 + s + 1], scalar2=None, op0=ALU.mult), r=[R_xr[s], R_ssq[g]], w=[R_h2[hk]])
                for half in range(2):
                    b = nb()
                    psb = ps[b][:].bitcast(BF16)
                    for kk in range(8):
                        kc = half * 8 + kk
                        P.op("pe", lambda e, psb=psb, kk=kk, kc=kc, hk=hk: e.transpose(out=psb[:, kk * 128:(kk + 1) * 128], in_=h2[hk][:, kc * 128:(kc + 1) * 128], identity=idb[:]),
                             r=[R_h2[hk], R_id], w=[RP[b]])
                    evac_copy(h2T[:, half * 8:(half + 1) * 8, s * 128:(s + 1) * 128], psb[:, 0:1024].rearrange("p (k t) -> p k t", t=128), r=[RP[b]], w=[R_h2T])
            for i in range(16):
                wi = wr_c[0]
                wr_c[0] += 1
                wr_ensure(wi + 2)
                if i == 12:
                    wd_ensure(wd_c[0] + 1)
                wk = wi % 2
                for fl in range(4):
                    f = i * 4 + fl
                    b = nb()
                    for kc in range(16):
                        P.op("pe", lambda e, b=b, kc=kc, fl=fl, wk=wk: e.matmul(ps[b][:], lhsT=wring[wk][:, kc, fl * 128:(fl + 1) * 128], rhs=h2T[:, kc, :],
                                                                             start=(kc == 0), stop=(kc == 15)), r=[R_wr[wk], R_h2T], w=[RP[b]])
                    P.op("act", lambda e, b=b, f=f: e.activation(out=aT[:, f, :], in_=ps[b][:], func=AF.Relu), r=[RP[b]], w=[R_aTc[f]])
                    P.op("pool", lambda e, f=f: e.tensor_tensor(out=aT[:, f, :], in0=aT[:, f, :], in1=aT[:, f, :], op=ALU.mult), r=[R_aTc[f]], w=[R_aTc[f]])
            for cb in range(4):
                bs4 = [nb() for _ in range(4)]
                for fg in range(8):
                    wi = wd_c[0]
                    wd_c[0] += 1
                    wd_ensure(wi + 2)
                    wk = wi % 2
                    for fl in range(8):
                        f = fg * 8 + fl
                        for s in range(4):
                            b = bs4[s]
                            P.op("pe", lambda e, b=b, f=f, fl=fl, s=s, wk=wk: e.matmul(ps[b][:], lhsT=aT[:, f, s * 128:(s + 1) * 128], rhs=wdr[wk][:, fl, :],
                                                                                     start=(f == 0), stop=(f == 63)), r=[R_aTc[f], R_wd[wk]], w=[RP[b]])
                for s in range(4):
                    xs = xr[s][:, cb * 512:(cb + 1) * 512]
                    b = bs4[s]
                    P.op("dve", lambda e, b=b, xs=xs: e.tensor_tensor(out=xs, in0=ps[b][:], in1=xs, op=ALU.add), r=[RP[b], R_xr[s]], w=[R_xr[s]])
            g = new_ssq_grp()
            for s in range(4):
                P.op("act", lambda e, s=s, g=g: e.activation(out=junk[:], in_=xr[s][:], func=AF.Square, accum_out=ssq[:, g * 4 + s:g * 4 + s + 1]),
                     r=[R_xr[s]], w=[R_junk, R_ssq[g]])
            rstd_chain(g, 4, D)
            for s in range(4):
                P.op("dve", lambda e, s=s, g=g: e.scalar_tensor_tensor(
                    out=xr[s][:], in0=xr[s][:], scalar=ssq[:, g * 4 + s:g * 4 + s + 1], in1=gfin[:], op0=ALU.mult, op1=ALU.mult),
                    r=[R_xr[s], R_ssq[g], R_gfin], w=[R_xr[s]])
                r0 = j * T + s * 128
                P.dma("pool", out[r0:r0 + 128, :], xr[s][:], r=[R_xr[s]], sem="out")
        P.barrier(["sp", "pool"])
        P.emit()
    return nc


_NC_CACHE = {}


def _host_tables(seq_start):
    slopes = 2.0 ** (-8.0 * np.arange(1, 9, dtype=np.float64) / 8)
    k = np.arange(128)[:, None]
    q = np.arange(128)[None, :]
    tab = np.zeros((128, 8, 3, 2, 128), np.float32)
    tabf = np.zeros((128, 8, 3, 128), np.float32)
    NEG = -30000.0
    for hd in range(8):
        for p, d in enumerate(PATS):
            j0 = q - k + 128
            t0 = np.where(k >= q, -slopes[hd] * d * j0, NEG)
            j1 = q - k
            t1 = np.where(k <= q, -slopes[hd] * d * j1, NEG)
            tab[:, hd, p, 0, :] = t0
            tab[:, hd, p, 1, :] = t1
            tabf[:, hd, p, :] = NEG if seq_start else t0
    invc = np.zeros((128, 4, T), np.float32)
    tpos = np.arange(T, dtype=np.float64)
    for g in range(4):
        w = 2 ** (g + 1)
        if seq_start:
            invc[:, g, :] = (1.0 / np.minimum(tpos + 1, w))[None, :]
        else:
            invc[:, g, :] = 1.0 / w
    return tab, tabf, invc


def kernel(x, norm_mix_g, w_in, pool_w, pool_scale, pool_out_norm_g, attn_out_norm_g,
           w_out, norm_mlp_g, w_up, w_down, norm_final_g):
    x = np.asarray(x, np.float32)
    B, S, _ = x.shape
    n = 8
    per = S // NOWN
    if "nc" not in _NC_CACHE:
        _NC_CACHE["nc"] = build()
    nc = _NC_CACHE["nc"]

    def col(v):
        return np.ascontiguousarray(np.asarray(v, np.float32).reshape(-1, 128).T)

    vecs = np.zeros((128, 64), np.float32)
    vecs[:, 0:16] = col(norm_mix_g[0])
    vecs[:, 16:32] = col(norm_mlp_g[0])
    vecs[:, 32:40] = col(pool_out_norm_g[0])
    vecs[:, 40:48] = col(attn_out_norm_g[0])
    vecs[:, 48:56] = col(pool_scale[0])
    gfin = np.ascontiguousarray(np.broadcast_to(np.asarray(norm_final_g, np.float32)[None, :], (128, D)))
    idf = np.eye(128, dtype=np.float32)
    shared = {
        "w_in": np.ascontiguousarray(np.asarray(w_in, np.float32)[0]),
        "pool_w": np.ascontiguousarray(np.asarray(pool_w, np.float32)[0]),
        "w_out": np.ascontiguousarray(np.asarray(w_out, np.float32)[0]),
        "w_up": np.ascontiguousarray(np.asarray(w_up, np.float32)[0]),
        "w_down": np.ascontiguousarray(np.asarray(w_down, np.float32)[0]),
        "vecs": vecs, "gfin": gfin, "idf": idf,
    }
    tabs = {True: _host_tables(True), False: _host_tables(False)}
    in_maps = []
    for c in range(n):
        b = c // per
        t0 = (c % per) * NOWN
        xh = np.zeros((NTOK, D), np.float32)
        if t0 == 0:
            xh[HALO:] = x[b, 0:NOWN]
        else:
            xh[:] = x[b, t0 - HALO:t0 + NOWN]
        tab, tabf, invc = tabs[t0 == 0]
        m = dict(shared)
        m.update({"xh": xh, "tab": tab, "tabf": tabf, "invc": invc})
        in_maps.append(m)
    res = run_bass_kernel_spmd(nc, in_maps, core_ids=list(range(n)))
    outp = np.empty((B, S, D), np.float32)
    for c in range(n):
        b = c // per
        t0 = (c % per) * NOWN
        outp[b, t0:t0 + NOWN] = res.results[c]["out"]
    return outp
```

```python
import numpy as np
from contextlib import ExitStack
import concourse.bass as bass
import concourse.mybir as mybir
from concourse.bass_utils import run_bass_kernel_spmd

F32 = mybir.dt.float32
BF16 = mybir.dt.bfloat16
AF = mybir.ActivationFunctionType
ALU = mybir.AluOpType

D = 2048
DFF = 8192
NOWN = 4096
HALO = 2048
NTOK = NOWN + HALO
T = 512
NT_ALL = NTOK // T
NT_H = HALO // T
NT_OWN = NOWN // T
EPS = 1e-6
QSCALE = 128 ** -0.5
PATS = (1, 4, 16)
SB_BASE = 20480
SB_LIMIT = 229376


class Res:
    __slots__ = ("name", "w", "rd")

    def __init__(self, name=""):
        self.name = name
        self.w = {}
        self.rd = {}


class Op:
    __slots__ = ("eng", "fn", "kind", "sem", "inc", "waits", "idx", "cnt")


class Prog:
    ENGS = ("pe", "act", "dve", "pool", "sp")

    def __init__(self, nc, stack):
        self.nc = nc
        self.stack = stack
        self.streams = {e: [] for e in self.ENGS}
        self.esem = {e: stack.enter_context(nc.semaphore("es_" + e)) for e in self.ENGS}
        self.dsem = {}
        self.dma_tot = {}
        self.last_c = {}

    def op(self, eng, fn, r=(), w=(), kind="c", sem=None):
        o = Op()
        o.eng = eng
        o.fn = fn
        o.kind = kind
        o.sem = sem
        o.inc = False
        o.waits = {}
        o.cnt = 0
        deps = []
        for x in r:
            for wr in x.w.values():
                deps.append((wr, True))
        for x in w:
            for wr in x.w.values():
                deps.append((wr, False))
            for rd in x.rd.values():
                deps.append((rd, False))
        for d, raw in deps:
            if d is o:
                continue
            if d.kind == "d":
                key = ("d", d.sem)
                o.waits[key] = max(o.waits.get(key, 0), self.dma_tot[d.sem])
            else:
                if kind == "c" and d.eng == eng:
                    if eng == "pe" or not raw:
                        continue
                d.inc = True
                key = ("c", d.eng)
                prev = o.waits.get(key)
                if prev is None or d.idx > prev.idx:
                    o.waits[key] = d
        o.idx = len(self.streams[eng])
        self.streams[eng].append(o)
        for x in r:
            x.rd[(eng, kind, sem)] = o
        for x in w:
            x.w[(eng, kind, sem)] = o
            x.rd = {}
        if kind == "d":
            self.dma_tot[sem] += 16
        else:
            self.last_c[eng] = o
        return o

    def dma(self, q, out, in_, r=(), w=(), sem="g"):
        if sem not in self.dsem:
            self.dsem[sem] = self.stack.enter_context(self.nc.semaphore("ds_" + sem))
            self.dma_tot[sem] = 0
        return self.op(q, lambda e: e.dma_start(out=out, in_=in_), r, w, kind="d", sem=sem)

    def barrier(self, engs=None):
        for e in (engs or self.ENGS):
            o = Op()
            o.eng = e
            o.fn = None
            o.kind = "c"
            o.sem = None
            o.inc = False
            o.cnt = 0
            o.waits = {}
            for g, l in self.last_c.items():
                if not (g == e and e == "pe"):
                    l.inc = True
                    o.waits[("c", g)] = l
            for s, t in self.dma_tot.items():
                if t:
                    o.waits[("d", s)] = t
            o.idx = len(self.streams[e])
            self.streams[e].append(o)

    def emit(self):
        for eng, ops in self.streams.items():
            c = 0
            for o in ops:
                if o.kind == "c" and o.inc:
                    c += 1
                o.cnt = c
        P = self

        def run(name, e):
            waited = {}
            for o in P.streams[name]:
                for key, v in o.waits.items():
                    if key[0] == "d":
                        sem = P.dsem[key[1]]
                        val = v
                    else:
                        sem = P.esem[key[1]]
                        val = v.cnt
                    if waited.get(key, 0) < val:
                        e.wait_ge(sem, val)
                        waited[key] = val
                if o.fn is None:
                    continue
                ins = o.fn(e)
                if o.kind == "d":
                    ins.then_inc(P.dsem[o.sem], 16)
                elif o.inc:
                    ins.then_inc(P.esem[name], 1)

        with self.nc.Block() as block:
            block.tensor(lambda e: run("pe", e))
            block.scalar(lambda e: run("act", e))
            block.vector(lambda e: run("dve", e))
            block.gpsimd(lambda e: run("pool", e))
            block.sync(lambda e: run("sp", e))


def build(stop_after=99, dbg=False, skip0=False):
    nc = bass.Bass("TRN2", target_bir_lowering=False)
    dk = "ExternalOutput" if dbg else "Internal"

    def din(name, shape, dt=F32):
        return nc.dram_tensor(name, shape, dt, kind="ExternalInput").ap()

    xh = din("xh", [NTOK, D])
    w_in = din("w_in", [D, 4096])
    pool_w = din("pool_w", [4, 256, 256])
    if not skip0:
        w_out = din("w_out", [D, D])
        w_up = din("w_up", [D, DFF])
        w_down = din("w_down", [DFF, D])
    vecs = din("vecs", [128, 64])
    gfin_d = din("gfin", [128, D])
    tab_d = din("tab", [128, 8, 3, 2, 128])
    tabf_d = din("tabf", [128, 8, 3, 128])
    invc_d = din("invc", [128, 4, T])
    idf_d = din("idf", [128, 128])
    out = nc.dram_tensor("out", [NOWN, D], F32, kind="ExternalOutput").ap()

    wup_s = nc.dram_tensor("wup_s", [16, 128, 16, 512], BF16).ap()
    wdn_s = nc.dram_tensor("wdn_s", [4, 8, 128, 8, 512], BF16).ap()
    wout_s = nc.dram_tensor("wout_s", [4, 128, 16, 512], BF16).ap()
    qT_s = nc.dram_tensor("qT_s", [8, 128, NOWN], BF16, kind=dk).ap()
    kT_s = nc.dram_tensor("kT_s", [8, 128, NTOK], BF16, kind=dk).ap()
    v_s = nc.dram_tensor("v_s", [NTOK, 1024], BF16, kind=dk).ap()
    aT_s = nc.dram_tensor("aT_s", [8, 128, NOWN], BF16, kind=dk).ap()
    bT_s = nc.dram_tensor("bT_s", [8, 128, NOWN], BF16, kind=dk).ap()

    with ExitStack() as stack:
        P = Prog(nc, stack)
        ps = [stack.enter_context(nc.psum_tensor("ps%d" % i, [128, 512], F32)) for i in range(8)]
        RP = [Res("ps%d" % i) for i in range(8)]
        bank = [0]

        def nb():
            b = bank[0]
            bank[0] = (b + 1) % 8
            return b

        off = [SB_BASE]
        ntens = [0]

        def sb(shape, dt, name="t"):
            nbytes = int(np.prod(shape[1:])) * (4 if dt == F32 else 2)
            ntens[0] += 1
            t = nc.alloc_sbuf_tensor_at("%s_%d" % (name, ntens[0]), shape, dt, offset=off[0])
            off[0] += (nbytes + 63) // 64 * 64
            assert off[0] <= SB_LIMIT, (name, off[0])
            return t

        idf_s = sb([128, 128], F32, "idf")
        idb = sb([128, 128], BF16, "idb")
        onesb = sb([128, 128], BF16, "ones")
        vec_s = sb([128, 64], F32, "vecs")
        ssq = sb([128, 64], F32, "ssq")
        junk = sb([128, D], BF16, "junk")
        R_id = Res()
        R_vec = Res()
        R_junk = Res()
        R_ssq = [Res() for _ in range(16)]
        R_ssqc = [Res() for _ in range(64)]
        ssq_ctr = [0]
        P.dma("sp", idf_s[:], idf_d, w=[R_id], sem="c0")
        P.dma("sp", vec_s[:], vecs, w=[R_vec], sem="c0")
        P.op("dve", lambda e: e.tensor_copy(out=idb[:], in_=idf_s[:]), r=[R_id], w=[R_id])
        P.op("dve", lambda e: e.memset(onesb[:], 1.0), w=[R_id])
        base_persist = off[0]

        def rstd_chain(grp, ncol, scale_div):
            c0 = grp * 4
            R = R_ssq[grp]
            P.op("dve", lambda e: e.tensor_scalar(out=ssq[:, c0:c0 + ncol], in0=ssq[:, c0:c0 + ncol], scalar1=1.0 / scale_div,
                                                  scalar2=EPS, op0=ALU.mult, op1=ALU.add), r=[R], w=[R])
            P.op("act", lambda e: e.activation(out=ssq[:, c0:c0 + ncol], in_=ssq[:, c0:c0 + ncol], func=AF.Sqrt), r=[R], w=[R])
            P.op("dve", lambda e: e.reciprocal(out=ssq[:, c0:c0 + ncol], in_=ssq[:, c0:c0 + ncol]), r=[R], w=[R])

        def rstd_chain_col(col, scale_div):
            R = R_ssqc[col]
            P.op("dve", lambda e: e.tensor_scalar(out=ssq[:, col:col + 1], in0=ssq[:, col:col + 1], scalar1=1.0 / scale_div,
                                                  scalar2=EPS, op0=ALU.mult, op1=ALU.add), r=[R], w=[R])
            P.op("act", lambda e: e.activation(out=ssq[:, col:col + 1], in_=ssq[:, col:col + 1], func=AF.Sqrt), r=[R], w=[R])
            P.op("dve", lambda e: e.reciprocal(out=ssq[:, col:col + 1], in_=ssq[:, col:col + 1]), r=[R], w=[R])

        def new_ssq_grp():
            g = ssq_ctr[0] % 16
            ssq_ctr[0] += 1
            c0 = g * 4
            P.op("dve", lambda e: e.memset(ssq[:, c0:c0 + 4], 0.0), w=[R_ssq[g]])
            return g

        evac_tog = [0]

        def evac_copy(out_ap, in_ap, r, w, scale=None):
            evac_tog[0] ^= 1
            if evac_tog[0]:
                if scale is None:
                    P.op("act", lambda e: e.activation(out=out_ap, in_=in_ap, func=AF.Copy), r=r, w=w)
                else:
                    P.op("act", lambda e: e.activation(out=out_ap, in_=in_ap, func=AF.Copy, scale=scale), r=r, w=w)
            else:
                if scale is None:
                    P.op("dve", lambda e: e.tensor_copy(out=out_ap, in_=in_ap), r=r, w=w)
                else:
                    P.op("dve", lambda e: e.tensor_scalar(out=out_ap, in0=in_ap, scalar1=scale, scalar2=None, op0=ALU.mult), r=r, w=w)

        off[0] = base_persist
        NST = 3
        stg_f = [sb([128, D], F32, "stgf") for _ in range(NST)]
        stg_b = [sb([128, D], BF16, "stgb") for _ in range(NST)]
        R_sf = [Res() for _ in range(NST)]
        R_sb = [Res() for _ in range(NST)]
        R_wup, R_wdn, R_wout = Res(), Res(), Res()
        jobs = []
        for kc in range(0 if skip0 else 16):
            for qd in range(4):
                jobs.append((w_up[kc * 128:(kc + 1) * 128, qd * 2048:(qd + 1) * 2048],
                             wup_s[qd * 4:(qd + 1) * 4, :, kc, :].rearrange("i p c -> p i c"), 16 + kc, R_wup))
        for f in range(0 if skip0 else 64):
            jobs.append((w_down[f * 128:(f + 1) * 128, :], wdn_s[:, f // 8, :, f % 8, :].rearrange("cb p c -> p cb c"), None, R_wdn))
        for kc in range(0 if skip0 else 16):
            jobs.append((w_out[kc * 128:(kc + 1) * 128, :], wout_s[:, :, kc, :].rearrange("cb p c -> p cb c"), 32 + kc, R_wout))
        cast_engs = ("dve", "act")

        def p0_load(i):
            src, dst, gcol, Rw = jobs[i]
            P.dma("sp", stg_f[i % NST][:], src, w=[R_sf[i % NST]], sem="sf%d" % (i % NST))

        def p0_cast(i):
            src, dst, gcol, Rw = jobs[i]
            k = i % NST
            eng = cast_engs[i % 2]
            o_ap = stg_b[k][:]
            i_ap = stg_f[k][:]
            if gcol is not None:
                sc = vec_s[:, gcol:gcol + 1]
                if eng == "act":
                    P.op("act", lambda e: e.activation(out=o_ap, in_=i_ap, func=AF.Copy, scale=sc), r=[R_sf[k], R_vec], w=[R_sb[k]])
                else:
                    P.op(eng, lambda e: e.tensor_scalar(out=o_ap, in0=i_ap, scalar1=sc, scalar2=None, op0=ALU.mult), r=[R_sf[k], R_vec], w=[R_sb[k]])
            else:
                if eng == "act":
                    P.op("act", lambda e: e.activation(out=o_ap, in_=i_ap, func=AF.Copy), r=[R_sf[k]], w=[R_sb[k]])
                else:
                    P.op(eng, lambda e: e.tensor_copy(out=o_ap, in_=i_ap), r=[R_sf[k]], w=[R_sb[k]])
            P.dma("sp", dst, stg_b[k][:].rearrange("p (i c) -> p i c", c=512), r=[R_sb[k]], w=[Rw], sem="sb%d" % k)

        if skip0:
            jobs = []
        for i in range(min(NST, len(jobs))):
            p0_load(i)
        for i in range(len(jobs)):
            p0_cast(i)
            if i + NST < len(jobs):
                p0_load(i + NST)
        P.barrier()
        if stop_after <= 0:
            P.barrier(["sp"])
            P.emit()
            return nc

        off[0] = base_persist
        Wres = sb([128, 16, 2048], BF16, "wres")
        R_W = Res()
        wst = [sb([128, 2048], F32, "wst") for _ in range(2)]
        R_wst = [Res(), Res()]
        NXR = 3
        xt = [sb([128, D], F32, "xt") for _ in range(NXR)]
        R_xt = [Res() for _ in range(NXR)]
        NHR = 3
        hb = [sb([128, D], BF16, "hb") for _ in range(NHR)]
        R_hb = [Res() for _ in range(NHR)]
        hT = [sb([128, 16, T], BF16, "hT")]
        R_hT = [Res(), Res()]
        base_p1 = off[0]
        hT.append(sb([128, 16, T], BF16, "hT"))

        def load_wres(col0, qscale_cols=None):
            for kc in range(16):
                k = kc % 2
                P.dma("sp", wst[k][:], w_in[kc * 128:(kc + 1) * 128, col0:col0 + 2048], w=[R_wst[k]], sem="wst%d" % k)
                sc = vec_s[:, kc:kc + 1]
                if qscale_cols is None:
                    if kc % 2:
                        P.op("dve", lambda e, kc=kc, k=k, sc=sc: e.tensor_scalar(
                            out=Wres[:, kc, :], in0=wst[k][:], scalar1=sc, scalar2=None, op0=ALU.mult), r=[R_wst[k], R_vec], w=[R_W])
                    else:
                        P.op("act", lambda e, kc=kc, k=k, sc=sc: e.activation(
                            out=Wres[:, kc, :], in_=wst[k][:], func=AF.Copy, scale=sc), r=[R_wst[k], R_vec], w=[R_W])
                else:
                    a, b = qscale_cols
                    P.op("act", lambda e, kc=kc, k=k, sc=sc: e.activation(
                        out=Wres[:, kc, 0:a], in_=wst[k][:, 0:a], func=AF.Copy, scale=sc), r=[R_wst[k], R_vec], w=[R_W])
                    P.op("dve", lambda e, kc=kc, k=k, sc=sc: e.tensor_scalar(
                        out=Wres[:, kc, a:b], in0=wst[k][:, a:b], scalar1=sc, scalar2=QSCALE, op0=ALU.mult, op1=ALU.mult), r=[R_wst[k], R_vec], w=[R_W])

        hctr = [0]
        xjobs = []
        xiss = [0]
        xcons = [0]

        def xload_ensure(n):
            while xiss[0] < min(n, len(xjobs)):
                i = xiss[0]
                k = i % NXR
                r0 = xjobs[i]
                P.dma("sp", xt[k][:], xh[r0:r0 + 128, :], w=[R_xt[k]], sem="xt%d" % k)
                xiss[0] += 1

        def front(j, hTi):
            g = new_ssq_grp()
            for s in range(4):
                i = xcons[0]
                xcons[0] += 1
                assert xjobs[i] == j * T + s * 128
                xload_ensure(i + NXR)
                k = i % NXR
                col = g * 4 + s
                Rc = R_ssqc[col]
                P.op("act", lambda e, k=k, col=col: e.activation(out=junk[:], in_=xt[k][:], func=AF.Square, accum_out=ssq[:, col:col + 1]),
                     r=[R_xt[k], R_ssq[g]], w=[R_junk, Rc])
                rstd_chain_col(col, D)
                hk = hctr[0] % NHR
                hctr[0] += 1
                if s % 2 == 0:
                    P.op("dve", lambda e, k=k, col=col, hk=hk: e.tensor_scalar(
                        out=hb[hk][:], in0=xt[k][:], scalar1=ssq[:, col:col + 1], scalar2=None, op0=ALU.mult),
                        r=[R_xt[k], Rc], w=[R_hb[hk]])
                else:
                    P.op("act", lambda e, k=k, col=col, hk=hk: e.activation(
                        out=hb[hk][:], in_=xt[k][:], func=AF.Copy, scale=ssq[:, col:col + 1]),
                        r=[R_xt[k], Rc], w=[R_hb[hk]])
                for half in range(2):
                    b = nb()
                    psb = ps[b][:].bitcast(BF16)
                    for kk in range(8):
                        kc = half * 8 + kk
                        P.op("pe", lambda e, psb=psb, kk=kk, kc=kc, hk=hk: e.transpose(
                            out=psb[:, kk * 128:(kk + 1) * 128], in_=hb[hk][:, kc * 128:(kc + 1) * 128], identity=idb[:]),
                            r=[R_hb[hk], R_id], w=[RP[b]])
                    evac_copy(hT[hTi][:, half * 8:(half + 1) * 8, s * 128:(s + 1) * 128],
                              psb[:, 0:1024].rearrange("p (k t) -> p k t", t=128), r=[RP[b]], w=[R_hT[hTi]])

        kst = [sb([128, 8, T], BF16, "kst") for _ in range(2)]
        vst = [sb([128, 4, 1024], BF16, "vst") for _ in range(2)]
        R_kst = [Res(), Res()]
        R_vst = [Res(), Res()]
        R_kT, R_v, R_qT, R_aT, R_bT = Res(), Res(), Res(), Res(), Res()
        load_wres(2048)
        xjobs.extend(j * T + s * 128 for j in range(NT_ALL) for s in range(4))
        for j in range(NT_ALL):
            hi = j % 2
            front(j, hi)
            k2 = j % 2
            for hd in range(8):
                b = nb()
                for kc in range(16):
                    P.op("pe", lambda e, b=b, kc=kc, hd=hd, hi=hi: e.matmul(ps[b][:], lhsT=Wres[:, kc, hd * 128:(hd + 1) * 128], rhs=hT[hi][:, kc, :],
                                                                         start=(kc == 0), stop=(kc == 15)), r=[R_W, R_hT[hi]], w=[RP[b]])
                evac_copy(kst[k2][:, hd, :], ps[b][:], r=[RP[b]], w=[R_kst[k2]])
            P.dma("sp", kT_s[:, :, j * T:(j + 1) * T].rearrange("h p t -> p h t"), kst[k2][:], r=[R_kst[k2]], w=[R_kT], sem="kst%d" % k2)
            for s in range(4):
                for half in range(2):
                    b = nb()
                    for kc in range(16):
                        P.op("pe", lambda e, b=b, kc=kc, s=s, half=half, hi=hi: e.matmul(
                            ps[b][:], lhsT=hT[hi][:, kc, s * 128:(s + 1) * 128], rhs=Wres[:, kc, 1024 + half * 512:1024 + (half + 1) * 512],
                            start=(kc == 0), stop=(kc == 15)), r=[R_W, R_hT[hi]], w=[RP[b]])
                    evac_copy(vst[k2][:, s, half * 512:(half + 1) * 512], ps[b][:], r=[RP[b]], w=[R_vst[k2]])
            P.dma("sp", v_s[j * T:(j + 1) * T, :].rearrange("(s p) c -> p s c", p=128), vst[k2][:], r=[R_vst[k2]], w=[R_v], sem="vst%d" % k2)
        P.barrier()
        if stop_after <= 1:
            P.barrier(["sp"])
            P.emit()
            return nc

        off[0] = base_p1
        U = sb([128, 8, 528], F32, "U")
        SA = sb([128, 2, 528], F32, "SA")
        SBb = sb([128, 2, 528], F32, "SB")
        dT = sb([128, 8, T], BF16, "dT")
        qst = sb([128, 8, T], BF16, "qst")
        ast = sb([128, 8, T], BF16, "ast")
        invc = sb([128, 4, T], F32, "invc")
        pwf = wst[0][:].rearrange("p (g c f) -> p g c f", g=4, c=2)
        pwb = sb([128, 4, 2, 256], BF16, "pwb")
        R_U, R_SA, R_SB, R_dT, R_qst, R_ast, R_invc, R_pw = Res(), Res(), Res(), Res(), Res(), Res(), Res(), Res()
        P.dma("sp", invc[:], invc_d, w=[R_invc], sem="c1")
        P.dma("sp", pwf, pool_w.rearrange("g (cc p) f -> p g cc f", p=128), w=[R_wst[0]], sem="wst0")
        P.op("dve", lambda e: e.tensor_copy(out=pwb[:], in_=pwf), r=[R_wst[0]], w=[R_pw])
        load_wres(0, qscale_cols=(1024, 2048))

        def y_mm(j):
            for g in range(4):
                for fc in range(2):
                    b = nb()
                    for cc in range(2):
                        P.op("pe", lambda e, b=b, g=g, fc=fc, cc=cc: e.matmul(ps[b][:], lhsT=pwb[:, g, cc, fc * 128:(fc + 1) * 128], rhs=dT[:, 2 * g + cc, :],
                                                                             start=(cc == 0), stop=(cc == 1)), r=[R_pw, R_dT], w=[RP[b]])
                    ch = 2 * g + fc
                    P.op("act", lambda e, b=b, ch=ch: e.activation(out=ast[:, ch, :], in_=ps[b][:], func=AF.Copy, scale=vec_s[:, 48 + ch:49 + ch]),
                         r=[RP[b], R_vec], w=[R_ast])
            P.dma("sp", aT_s[:, :, (j - NT_H) * T:(j - NT_H + 1) * T].rearrange("c p t -> p c t"), ast[:], r=[R_ast], w=[R_aT], sem="ast")

        xjobs.extend(j * T + s * 128 for j in range(NT_H - 1, NT_ALL) for s in range(4))
        for j in range(NT_H - 1, NT_ALL):
            front(j, 0)
            if j >= NT_H:
                for hd in range(8):
                    b = nb()
                    for kc in range(16):
                        P.op("pe", lambda e, b=b, kc=kc, hd=hd: e.matmul(ps[b][:], lhsT=Wres[:, kc, 1024 + hd * 128:1024 + (hd + 1) * 128], rhs=hT[0][:, kc, :],
                                                                     start=(kc == 0), stop=(kc == 15)), r=[R_W, R_hT[0]], w=[RP[b]])
                    evac_copy(qst[:, hd, :], ps[b][:], r=[RP[b]], w=[R_qst])
                P.dma("sp", qT_s[:, :, (j - NT_H) * T:(j - NT_H + 1) * T].rearrange("h p t -> p h t"), qst[:], r=[R_qst], w=[R_qT], sem="qst")
            if j > NT_H:
                y_mm(j - 1)
            for c in range(8):
                b = nb()
                for kc in range(16):
                    P.op("pe", lambda e, b=b, kc=kc, c=c: e.matmul(ps[b][:], lhsT=Wres[:, kc, c * 128:(c + 1) * 128], rhs=hT[0][:, kc, :],
                                                               start=(kc == 0), stop=(kc == 15)), r=[R_W, R_hT[0]], w=[RP[b]])
                evac_copy(U[:, c, 16:528], ps[b][:], r=[RP[b]], w=[R_U])
            if j >= NT_H:
                for g in range(4):
                    w_ = 2 ** (g + 1)
                    Ug = U[:, 2 * g:2 * g + 2, :]
                    P.op("pool", lambda e, Ug=Ug: e.tensor_tensor(out=SA[:, :, 2:528], in0=Ug[:, :, 2:528], in1=Ug[:, :, 1:527], op=ALU.add), r=[R_U], w=[R_SA])
                    Tr, R_T = SA, R_SA
                    if g >= 1:
                        P.op("pool", lambda e: e.tensor_tensor(out=SBb[:, :, 4:528], in0=SA[:, :, 4:528], in1=SA[:, :, 2:526], op=ALU.add), r=[R_SA], w=[R_SB])
                        Tr, R_T = SBb, R_SB
                    if g >= 2:
                        P.op("pool", lambda e: e.tensor_tensor(out=SA[:, :, 8:528], in0=SBb[:, :, 8:528], in1=SBb[:, :, 4:524], op=ALU.add), r=[R_SB], w=[R_SA])
                        Tr, R_T = SA, R_SA
                    if g >= 3:
                        P.op("pool", lambda e: e.tensor_tensor(out=SBb[:, :, 16:528], in0=SA[:, :, 16:528], in1=SA[:, :, 8:520], op=ALU.add), r=[R_SA], w=[R_SB])
                        Tr, R_T = SBb, R_SB
                    if j == NT_H:
                        for cc in range(2):
                            P.op("pool", lambda e, Tr=Tr, cc=cc, g=g: e.tensor_tensor(out=Tr[:, cc, 16:528], in0=Tr[:, cc, 16:528], in1=invc[:, g, :], op=ALU.mult),
                                 r=[R_T, R_invc], w=[R_T])
                        P.op("pool", lambda e, Tr=Tr, Ug=Ug, g=g: e.tensor_tensor(out=dT[:, 2 * g:2 * g + 2, :], in0=Tr[:, :, 16:528], in1=Ug[:, :, 16:528], op=ALU.subtract),
                             r=[R_T, R_U], w=[R_dT])
                    else:
                        P.op("dve", lambda e, Tr=Tr, Ug=Ug, g=g, w_=w_: e.scalar_tensor_tensor(out=dT[:, 2 * g:2 * g + 2, :], in0=Tr[:, :, 16:528], scalar=1.0 / w_,
                                                                                         in1=Ug[:, :, 16:528], op0=ALU.mult, op1=ALU.subtract),
                             r=[R_T, R_U], w=[R_dT])
            P.op("pool", lambda e: e.tensor_copy(out=U[:, :, 0:16], in_=U[:, :, 512:528]), r=[R_U], w=[R_U])
        y_mm(NT_ALL - 1)
        P.barrier()
        if stop_after <= 2:
            P.barrier(["sp"])
            P.emit()
            return nc

        off[0] = base_persist
        QT = [sb([128, NOWN], BF16, "QT") for _ in range(2)]
        KT = [sb([128, NTOK], BF16, "KT") for _ in range(2)]
        V3 = [sb([128, 3, 48, 128], BF16, "V3") for _ in range(2)]
        tabs = [sb([128, 3, 2, 128], F32, "tab") for _ in range(2)]
        tabf = [sb([128, 3, 128], F32, "tabf") for _ in range(2)]
        ND = sb([128, 2, NOWN], F32, "ND")
        BTs = [sb([128, NOWN], BF16, "BTs") for _ in range(2)]
        NSR = 3
        Sb = [sb([128, 512], F32, "Sb") for _ in range(NSR)]
        PT = [sb([128, 512], BF16, "PT") for _ in range(NSR)]
        R_QT, R_KT, R_V3, R_tab = [Res(), Res()], [Res(), Res()], [Res(), Res()], [Res(), Res()]
        R_ND = Res()
        R_BTs = [Res(), Res()]
        R_Sb = [Res() for _ in range(NSR)]
        R_PT = [Res() for _ in range(NSR)]

        def p2_load(hd):
            pr = hd % 2
            P.dma("sp", QT[pr][:], qT_s[hd], r=[R_qT], w=[R_QT[pr]], sem="QT%d" % pr)
            P.dma("sp", KT[pr][:], kT_s[hd], r=[R_kT], w=[R_KT[pr]], sem="KT%d" % pr)
            P.dma("sp", tabs[pr][:], tab_d[:, hd], w=[R_tab[pr]], sem="tab%d" % pr)
            P.dma("sp", tabf[pr][:], tabf_d[:, hd], w=[R_tab[pr]], sem="tab%d" % pr)
            for p, d in enumerate(PATS):
                M = 48 // d
                vv = v_s.rearrange("(m i r) c -> r m i c", r=d, i=128)
                if d == 16:
                    for r16 in range(16):
                        src = vv[r16, :, :, hd * 128:(hd + 1) * 128].rearrange("m i c -> i m c")
                        dst = V3[pr][:, p, r16 * 3:(r16 + 1) * 3, :]
                        P.dma("sp", dst, src, r=[R_v], w=[R_V3[pr]], sem="V3%d" % pr)
                    continue
                for q4 in range(4):
                    if d == 1:
                        src = vv[0, q4 * 12:(q4 + 1) * 12, :, hd * 128:(hd + 1) * 128].rearrange("m i c -> i m c")
                        dst = V3[pr][:, p, q4 * 12:(q4 + 1) * 12, :]
                    elif d == 4:
                        src = vv[q4, :, :, hd * 128:(hd + 1) * 128].rearrange("m i c -> i m c")
                        dst = V3[pr][:, p, q4 * 12:(q4 + 1) * 12, :]
                    else:
                        src = vv[q4 * 4:(q4 + 1) * 4, :, :, hd * 128:(hd + 1) * 128].rearrange("r m i c -> i r m c")
                        dst = V3[pr][:, p, q4 * 12:(q4 + 1) * 12, :].rearrange("i (r m) c -> i r m c", m=3)
                    P.dma("sp", dst, src, r=[R_v], w=[R_V3[pr]], sem="V3%d" % pr)

        sctr = [0]
        p2_load(0)
        for hd in range(8):
            pr = hd % 2
            if hd + 1 < 8:
                p2_load(hd + 1)
            for p, d in enumerate(PATS):
                M = 48 // d
                m0 = (HALO // 128) // d
                nqb = 32 // d
                KTv = KT[pr][:].rearrange("p (m i r) -> p r m i", r=d, i=128)
                QTv = QT[pr][:].rearrange("p (m i r) -> p r m i", r=d, i=128)
                NDv = ND[:].rearrange("p n (m i r) -> p n r m i", r=d, i=128)
                for r_ in range(d):
                    for mm0 in range(0, nqb, 2):
                        sr = sctr[0] % NSR
                        sctr[0] += 1
                        bs = nb()
                        for qb in range(2):
                            m = m0 + mm0 + qb
                            for c in range(2):
                                P.op("pe", lambda e, bs=bs, qb=qb, c=c, m=m, r_=r_, mm0=mm0, KTv=KTv, QTv=QTv: e.matmul(
                                    ps[bs][:, (qb * 2 + c) * 128:(qb * 2 + c + 1) * 128], lhsT=KTv[:, r_, m - 1 + c, :], rhs=QTv[:, r_, mm0 + qb, :],
                                    start=True, stop=True), r=[R_KT[pr], R_QT[pr]], w=[RP[bs]])
                        for qb in range(2):
                            if mm0 + qb == 0:
                                P.op("dve", lambda e, bs=bs, qb=qb, sr=sr, p=p, pr=pr: e.tensor_tensor(
                                    out=Sb[sr][:, qb * 256:qb * 256 + 128], in0=ps[bs][:, qb * 256:qb * 256 + 128], in1=tabf[pr][:, p, :], op=ALU.add),
                                    r=[RP[bs], R_tab[pr]], w=[R_Sb[sr]])
                                P.op("dve", lambda e, bs=bs, qb=qb, sr=sr, p=p, pr=pr: e.tensor_tensor(
                                    out=Sb[sr][:, qb * 256 + 128:qb * 256 + 256], in0=ps[bs][:, qb * 256 + 128:qb * 256 + 256], in1=tabs[pr][:, p, 1, :], op=ALU.add),
                                    r=[RP[bs], R_tab[pr]], w=[R_Sb[sr]])
                            else:
                                P.op("dve", lambda e, bs=bs, qb=qb, sr=sr, p=p, pr=pr: e.tensor_tensor(
                                    out=Sb[sr][:, qb * 256:(qb + 1) * 256], in0=ps[bs][:, qb * 256:(qb + 1) * 256],
                                    in1=tabs[pr][:, p, :, :].rearrange("k c q -> k (c q)"), op=ALU.add),
                                    r=[RP[bs], R_tab[pr]], w=[R_Sb[sr]])
                        P.op("act", lambda e, sr=sr: e.activation(out=PT[sr][:], in_=Sb[sr][:], func=AF.Exp), r=[R_Sb[sr]], w=[R_PT[sr]])
                        bo = nb()
                        for qb in range(2):
                            m = m0 + mm0 + qb
                            for c in range(2):
                                tau = r_ * M + m - 1 + c
                                P.op("pe", lambda e, bo=bo, qb=qb, c=c, tau=tau, sr=sr, p=p, pr=pr: e.matmul(
                                    ps[bo][:, qb * 128:(qb + 1) * 128], lhsT=V3[pr][:, p, tau, :], rhs=PT[sr][:, (qb * 2 + c) * 128:(qb * 2 + c + 1) * 128],
                                    start=(c == 0), stop=(c == 1)), r=[R_V3[pr], R_PT[sr]], w=[RP[bo]])
                            for c in range(2):
                                P.op("pe", lambda e, bo=bo, qb=qb, c=c, sr=sr: e.matmul(
                                    ps[bo][:, 256 + qb * 128:256 + (qb + 1) * 128], lhsT=onesb[:], rhs=PT[sr][:, (qb * 2 + c) * 128:(qb * 2 + c + 1) * 128],
                                    start=(c == 0), stop=(c == 1)), r=[R_id, R_PT[sr]], w=[RP[bo]])
                        for qb in range(2):
                            tgt = NDv[:, :, r_, mm0 + qb, :]
                            src = ps[bo][:].rearrange("p (n q i) -> p n q i", n=2, q=2)[:, :, qb, :]
                            if p == 0:
                                P.op("act", lambda e, tgt=tgt, src=src: e.activation(out=tgt, in_=src, func=AF.Copy), r=[RP[bo]], w=[R_ND])
                            else:
                                P.op("dve", lambda e, tgt=tgt, src=src: e.tensor_tensor(out=tgt, in0=src, in1=tgt, op=ALU.add), r=[RP[bo], R_ND], w=[R_ND])
            for q4 in range(4):
                sl = slice(q4 * 1024, (q4 + 1) * 1024)
                P.op("dve", lambda e, sl=sl: e.reciprocal(out=ND[:, 1, sl], in_=ND[:, 1, sl]), r=[R_ND], w=[R_ND])
                P.op("pool", lambda e, sl=sl, pr=pr: e.tensor_tensor(out=BTs[pr][:, sl], in0=ND[:, 0, sl], in1=ND[:, 1, sl], op=ALU.mult), r=[R_ND], w=[R_BTs[pr]])
            P.dma("sp", bT_s[hd], BTs[pr][:], r=[R_BTs[pr]], w=[R_bT], sem="BTs%d" % pr)
        P.barrier()
        if stop_after <= 3:
            P.barrier(["sp"])
            P.emit()
            return nc

        off[0] = base_persist
        xr = [sb([128, D], F32, "xr") for _ in range(4)]
        R_xr = [Res() for _ in range(4)]
        yT = sb([128, 16, T], BF16, "yT")
        R_yT = Res()
        ysq = [sb([128, T], BF16, "ysq") for _ in range(3)]
        R_ysq = [Res() for _ in range(3)]
        gfin = sb([128, D], F32, "gfin")
        R_gfin = Res()
        h2 = [sb([128, D], BF16, "h2") for _ in range(2)]
        R_h2 = [Res(), Res()]
        h2T = sb([128, 16, T], BF16, "h2T")
        R_h2T = Res()
        aT = sb([128, 64, T], BF16, "aT")
        R_aTc = [Res() for _ in range(64)]
        wring = [sb([128, 16, 512], BF16, "wring") for _ in range(2)]
        R_wr = [Res(), Res()]
        wdr = [sb([128, 8, 512], BF16, "wdr") for _ in range(2)]
        R_wd = [Res(), Res()]
        rpa = sb([128, 8], F32, "rpa")
        R_rpa = Res()
        P.dma("sp", gfin[:], gfin_d, w=[R_gfin], sem="c1")

        wr_list = []
        wd_list = []
        for j in range(NT_OWN):
            for cb in range(4):
                wr_list.append((wout_s[cb], R_wout))
            for i in range(16):
                wr_list.append((wup_s[i], R_wup))
            for cb in range(4):
                for fg in range(8):
                    wd_list.append((wdn_s[cb, fg], R_wdn))
        wr_iss = [0]
        wd_iss = [0]

        def wr_ensure(n):
            while wr_iss[0] < min(n, len(wr_list)):
                i = wr_iss[0]
                src, Rs = wr_list[i]
                P.dma("sp", wring[i % 2][:], src, r=[Rs], w=[R_wr[i % 2]], sem="wr%d" % (i % 2))
                wr_iss[0] += 1

        def wd_ensure(n):
            while wd_iss[0] < min(n, len(wd_list)):
                i = wd_iss[0]
                src, Rs = wd_list[i]
                P.dma("sp", wdr[i % 2][:], src, r=[Rs], w=[R_wd[i % 2]], sem="wd%d" % (i % 2))
                wd_iss[0] += 1

        wr_c = [0]
        wd_c = [0]
        yq = [0]
        h2c = [0]

        def p3_loads(j):
            P.dma("sp", yT[:, 0:8, :], aT_s[:, :, j * T:(j + 1) * T].rearrange("c p t -> p c t"), r=[R_aT], w=[R_yT], sem="yT")
            P.dma("sp", yT[:, 8:16, :], bT_s[:, :, j * T:(j + 1) * T].rearrange("c p t -> p c t"), r=[R_bT], w=[R_yT], sem="yT")
            for s in range(4):
                r0 = HALO + j * T + s * 128
                P.dma("sp", xr[s][:], xh[r0:r0 + 128, :], w=[R_xr[s]], sem="xr%d" % s)

        for j in range(NT_OWN):
            wr_ensure(wr_c[0] + 1)
            p3_loads(j)
            bq = nb()
            P.op("dve", lambda e, bq=bq: e.memset(ps[bq][:, 0:8], 0.0), w=[RP[bq]])
            for kc in range(16):
                yk = yq[0] % 3
                yq[0] += 1
                P.op("pool", lambda e, kc=kc, yk=yk: e.tensor_tensor(out=ysq[yk][:], in0=yT[:, kc, :], in1=yT[:, kc, :], op=ALU.mult), r=[R_yT], w=[R_ysq[yk]])
                for s in range(4):
                    col = (kc // 8) * 4 + s
                    P.op("pe", lambda e, bq=bq, yk=yk, s=s, col=col: e.matmul(ps[bq][:, col:col + 1], lhsT=ysq[yk][:, s * 128:(s + 1) * 128], rhs=onesb[:, 0:1],
                                                                            start=False, stop=False, skip_group_check=True), r=[R_ysq[yk], R_id], w=[RP[bq]])
            P.op("dve", lambda e, bq=bq: e.tensor_scalar(out=rpa[:], in0=ps[bq][:, 0:8], scalar1=1.0 / 1024, scalar2=EPS, op0=ALU.mult, op1=ALU.add), r=[RP[bq]], w=[R_rpa])
            P.op("act", lambda e: e.activation(out=rpa[:], in_=rpa[:], func=AF.Sqrt), r=[R_rpa], w=[R_rpa])
            P.op("dve", lambda e: e.reciprocal(out=rpa[:], in_=rpa[:]), r=[R_rpa], w=[R_rpa])
            for cb in range(4):
                wi = wr_c[0]
                wr_c[0] += 1
                wr_ensure(wi + 2)
                wk = wi % 2
                for s in range(4):
                    bp, ba = nb(), nb()
                    for kc in range(16):
                        b = bp if kc < 8 else ba
                        P.op("pe", lambda e, b=b, kc=kc, s=s, wk=wk: e.matmul(ps[b][:], lhsT=yT[:, kc, s * 128:(s + 1) * 128], rhs=wring[wk][:, kc, :],
                                                                            start=(kc % 8 == 0), stop=(kc % 8 == 7)), r=[R_yT, R_wr[wk]], w=[RP[b]])
                    xs = xr[s][:, cb * 512:(cb + 1) * 512]
                    P.op("dve", lambda e, bp=bp, s=s, xs=xs: e.scalar_tensor_tensor(out=xs, in0=ps[bp][:], scalar=rpa[:, s:s + 1], in1=xs, op0=ALU.mult, op1=ALU.add),
                         r=[RP[bp], R_rpa, R_xr[s]], w=[R_xr[s]])
                    P.op("dve", lambda e, ba=ba, s=s, xs=xs: e.scalar_tensor_tensor(out=xs, in0=ps[ba][:], scalar=rpa[:, 4 + s:5 + s], in1=xs, op0=ALU.mult, op1=ALU.add),
                         r=[RP[ba], R_rpa, R_xr[s]], w=[R_xr[s]])
            g = new_ssq_grp()
            for s in range(4):
                P.op("act", lambda e, s=s, g=g: e.activation(out=junk[:], in_=xr[s][:], func=AF.Square, accum_out=ssq[:, g * 4 + s:g * 4 + s + 1]),
                     r=[R_xr[s]], w=[R_junk, R_ssq[g]])
            rstd_chain(g, 4, D)
            for s in range(4):
                hk = h2c[0] % 2
                h2c[0] += 1
                if s % 2 == 0:
                    P.op("dve", lambda e, s=s, g=g, hk=hk: e.tensor_scalar(
                        out=h2[hk][:], in0=xr[s][:], scalar1=ssq[:, g * 4 + s:g * 4 + s + 1], scalar2=None, op0=ALU.mult), r=[R_xr[s], R_ssq[g]], w=[R_h2[hk]])
                else:
                    P.op("act", lambda e, s=s, g=g, hk=hk: e.activation(
                        out=h2[hk][:], in_=xr[s][:], func=AF.Copy, scale=ssq[:, g * 4 + s:g * 4 + s + 1]), r=[R_xr[s], R_ssq[g]], w=[R_h2[hk]])
                for half in range(2):
                    b = nb()
                    psb = ps[b][:].bitcast(BF16)
                    for kk in range(8):
                        kc = half * 8 + kk
                        P.op("pe", lambda e, psb=psb, kk=kk, kc=kc, hk=hk: e.transpose(out=psb[:, kk * 128:(kk + 1) * 128], in_=h2[hk][:, kc * 128:(kc + 1) * 128], identity=idb[:]),
                             r=[R_h2[hk], R_id], w=[RP[b]])
                    evac_copy(h2T[:, half * 8:(half + 1) * 8, s * 128:(s + 1) * 128], psb[:, 0:1024].rearrange("p (k t) -> p k t", t=128), r=[RP[b]], w=[R_h2T])
            for i in range(16):
                wi = wr_c[0]
                wr_c[0] += 1
                wr_ensure(wi + 2)
                if i == 12:
                    wd_ensure(wd_c[0] + 1)
                wk = wi % 2
                for fl in range(4):
                    f = i * 4 + fl
                    b = nb()
                    for kc in range(16):
                        P.op("pe", lambda e, b=b, kc=kc, fl=fl, wk=wk: e.matmul(ps[b][:], lhsT=wring[wk][:, kc, fl * 128:(fl + 1) * 128], rhs=h2T[:, kc, :],
                                                                             start=(kc == 0), stop=(kc == 15)), r=[R_wr[wk], R_h2T], w=[RP[b]])
                    P.op("act", lambda e, b=b, f=f: e.activation(out=aT[:, f, :], in_=ps[b][:], func=AF.Relu), r=[RP[b]], w=[R_aTc[f]])
                    P.op("pool", lambda e, f=f: e.tensor_tensor(out=aT[:, f, :], in0=aT[:, f, :], in1=aT[:, f, :], op=ALU.mult), r=[R_aTc[f]], w=[R_aTc[f]])
            for cb in range(4):
                bs4 = [nb() for _ in range(4)]
                for fg in range(8):
                    wi = wd_c[0]
                    wd_c[0] += 1
                    wd_ensure(wi + 2)
                    wk = wi % 2
                    for fl in range(8):
                        f = fg * 8 + fl
                        for s in range(4):
                            b = bs4[s]
                            P.op("pe", lambda e, b=b, f=f, fl=fl, s=s, wk=wk: e.matmul(ps[b][:], lhsT=aT[:, f, s * 128:(s + 1) * 128], rhs=wdr[wk][:, fl, :],
                                                                                     start=(f == 0), stop=(f == 63)), r=[R_aTc[f], R_wd[wk]], w=[RP[b]])
                for s in range(4):
                    xs = xr[s][:, cb * 512:(cb + 1) * 512]
                    b = bs4[s]
                    P.op("dve", lambda e, b=b, xs=xs: e.tensor_tensor(out=xs, in0=ps[b][:], in1=xs, op=ALU.add), r=[RP[b], R_xr[s]], w=[R_xr[s]])
            g = new_ssq_grp()
            for s in range(4):
                P.op("act", lambda e, s=s, g=g: e.activation(out=junk[:], in_=xr[s][:], func=AF.Square, accum_out=ssq[:, g * 4 + s:g * 4 + s + 1]),
                     r=[R_xr[s]], w=[R_junk, R_ssq[g]])
            rstd_chain(g, 4, D)
            for s in range(4):
                P.op("dve", lambda e, s=s, g=g: e.scalar_tensor_tensor(
                    out=xr[s][:], in0=xr[s][:], scalar=ssq[:, g * 4 + s:g * 4 + s + 1], in1=gfin[:], op0=ALU.mult, op1=ALU.mult),
                    r=[R_xr[s], R_ssq[g], R_gfin], w=[R_xr[s]])
                r0 = j * T + s * 128
                P.dma("pool", out[r0:r0 + 128, :], xr[s][:], r=[R_xr[s]], sem="out")
        P.barrier(["sp", "pool"])
        P.emit()
    return nc


_NC_CACHE = {}


def _host_tables(seq_start):
    slopes = 2.0 ** (-8.0 * np.arange(1, 9, dtype=np.float64) / 8)
    k = np.arange(128)[:, None]
    q = np.arange(128)[None, :]
    tab = np.zeros((128, 8, 3, 2, 128), np.float32)
    tabf = np.zeros((128, 8, 3, 128), np.float32)
    NEG = -30000.0
    for hd in range(8):
        for p, d in enumerate(PATS):
            j0 = q - k + 128
            t0 = np.where(k >= q, -slopes[hd] * d * j0, NEG)
            j1 = q - k
            t1 = np.where(k <= q, -slopes[hd] * d * j1, NEG)
            tab[:, hd, p, 0, :] = t0
            tab[:, hd, p, 1, :] = t1
            tabf[:, hd, p, :] = NEG if seq_start else t0
    invc = np.zeros((128, 4, T), np.float32)
    tpos = np.arange(T, dtype=np.float64)
    for g in range(4):
        w = 2 ** (g + 1)
        if seq_start:
            invc[:, g, :] = (1.0 / np.minimum(tpos + 1, w))[None, :]
        else:
            invc[:, g, :] = 1.0 / w
    return tab, tabf, invc


def kernel(x, norm_mix_g, w_in, pool_w, pool_scale, pool_out_norm_g, attn_out_norm_g,
           w_out, norm_mlp_g, w_up, w_down, norm_final_g):
    x = np.asarray(x, np.float32)
    B, S, _ = x.shape
    n = 8
    per = S // NOWN
    if "nc" not in _NC_CACHE:
        _NC_CACHE["nc"] = build()
    nc = _NC_CACHE["nc"]

    def col(v):
        return np.ascontiguousarray(np.asarray(v, np.float32).reshape(-1, 128).T)

    vecs = np.zeros((128, 64), np.float32)
    vecs[:, 0:16] = col(norm_mix_g[0])
    vecs[:, 16:32] = col(norm_mlp_g[0])
    vecs[:, 32:40] = col(pool_out_norm_g[0])
    vecs[:, 40:48] = col(attn_out_norm_g[0])
    vecs[:, 48:56] = col(pool_scale[0])
    gfin = np.ascontiguousarray(np.broadcast_to(np.asarray(norm_final_g, np.float32)[None, :], (128, D)))
    idf = np.eye(128, dtype=np.float32)
    shared = {
        "w_in": np.ascontiguousarray(np.asarray(w_in, np.float32)[0]),
        "pool_w": np.ascontiguousarray(np.asarray(pool_w, np.float32)[0]),
        "w_out": np.ascontiguousarray(np.asarray(w_out, np.float32)[0]),
        "w_up": np.ascontiguousarray(np.asarray(w_up, np.float32)[0]),
        "w_down": np.ascontiguousarray(np.asarray(w_down, np.float32)[0]),
        "vecs": vecs, "gfin": gfin, "idf": idf,
    }
    tabs = {True: _host_tables(True), False: _host_tables(False)}
    in_maps = []
    for c in range(n):
        b = c // per
        t0 = (c % per) * NOWN
        xh = np.zeros((NTOK, D), np.float32)
        if t0 == 0:
            xh[HALO:] = x[b, 0:NOWN]
        else:
            xh[:] = x[b, t0 - HALO:t0 + NOWN]
        tab, tabf, invc = tabs[t0 == 0]
        m = dict(shared)
        m.update({"xh": xh, "tab": tab, "tabf": tabf, "invc": invc})
        in_maps.append(m)
    res = run_bass_kernel_spmd(nc, in_maps, core_ids=list(range(n)))
    outp = np.empty((B, S, D), np.float32)
    for c in range(n):
        b = c // per
        t0 = (c % per) * NOWN
        outp[b, t0:t0 + NOWN] = res.results[c]["out"]
    return outp
```

```python
import numpy as np
from contextlib import ExitStack
import concourse.bass as bass
import concourse.mybir as mybir
from concourse.bass_utils import run_bass_kernel_spmd

F32 = mybir.dt.float32
BF16 = mybir.dt.bfloat16
AF = mybir.ActivationFunctionType
ALU = mybir.AluOpType

D = 2048
DFF = 8192
NOWN = 4096
HALO = 2048
NTOK = NOWN + HALO
T = 512
NT_ALL = NTOK // T
NT_H = HALO // T
NT_OWN = NOWN // T
EPS = 1e-6
QSCALE = 128 ** -0.5
PATS = (1, 4, 16)
SB_BASE = 20480
SB_LIMIT = 229376


class Res:
    __slots__ = ("name", "w", "rd")

    def __init__(self, name=""):
        self.name = name
        self.w = {}
        self.rd = {}


class Op:
    __slots__ = ("eng", "fn", "kind", "sem", "inc", "waits", "idx", "cnt")


class Prog:
    ENGS = ("pe", "act", "dve", "pool", "sp")

    def __init__(self, nc, stack):
        self.nc = nc
        self.stack = stack
        self.streams = {e: [] for e in self.ENGS}
        self.esem = {e: stack.enter_context(nc.semaphore("es_" + e)) for e in self.ENGS}
        self.dsem = {}
        self.dma_tot = {}
        self.last_c = {}

    def op(self, eng, fn, r=(), w=(), kind="c", sem=None):
        o = Op()
        o.eng = eng
        o.fn = fn
        o.kind = kind
        o.sem = sem
        o.inc = False
        o.waits = {}
        o.cnt = 0
        deps = []
        for x in r:
            for wr in x.w.values():
                deps.append((wr, True))
        for x in w:
            for wr in x.w.values():
                deps.append((wr, False))
            for rd in x.rd.values():
                deps.append((rd, False))
        for d, raw in deps:
            if d is o:
                continue
            if d.kind == "d":
                key = ("d", d.sem)
                o.waits[key] = max(o.waits.get(key, 0), self.dma_tot[d.sem])
            else:
                if kind == "c" and d.eng == eng:
                    if eng == "pe" or not raw:
                        continue
                d.inc = True
                key = ("c", d.eng)
                prev = o.waits.get(key)
                if prev is None or d.idx > prev.idx:
                    o.waits[key] = d
        o.idx = len(self.streams[eng])
        self.streams[eng].append(o)
        for x in r:
            x.rd[(eng, kind, sem)] = o
        for x in w:
            x.w[(eng, kind, sem)] = o
            x.rd = {}
        if kind == "d":
            self.dma_tot[sem] += 16
        else:
            self.last_c[eng] = o
        return o

    def dma(self, q, out, in_, r=(), w=(), sem="g"):
        if sem not in self.dsem:
            self.dsem[sem] = self.stack.enter_context(self.nc.semaphore("ds_" + sem))
            self.dma_tot[sem] = 0
        return self.op(q, lambda e: e.dma_start(out=out, in_=in_), r, w, kind="d", sem=sem)

    def barrier(self, engs=None):
        for e in (engs or self.ENGS):
            o = Op()
            o.eng = e
            o.fn = None
            o.kind = "c"
            o.sem = None
            o.inc = False
            o.cnt = 0
            o.waits = {}
            for g, l in self.last_c.items():
                if not (g == e and e == "pe"):
                    l.inc = True
                    o.waits[("c", g)] = l
            for s, t in self.dma_tot.items():
                if t:
                    o.waits[("d", s)] = t
            o.idx = len(self.streams[e])
            self.streams[e].append(o)

    def emit(self):
        for eng, ops in self.streams.items():
            c = 0
            for o in ops:
                if o.kind == "c" and o.inc:
                    c += 1
                o.cnt = c
        P = self

        def run(name, e):
            waited = {}
            for o in P.streams[name]:
                for key, v in o.waits.items():
                    if key[0] == "d":
                        sem = P.dsem[key[1]]
                        val = v
                    else:
                        sem = P.esem[key[1]]
                        val = v.cnt
                    if waited.get(key, 0) < val:
                        e.wait_ge(sem, val)
                        waited[key] = val
                if o.fn is None:
                    continue
                ins = o.fn(e)
                if o.kind == "d":
                    ins.then_inc(P.dsem[o.sem], 16)
                elif o.inc:
                    ins.then_inc(P.esem[name], 1)

        with self.nc.Block() as block:
            block.tensor(lambda e: run("pe", e))
            block.scalar(lambda e: run("act", e))
            block.vector(lambda e: run("dve", e))
            block.gpsimd(lambda e: run("pool", e))
            block.sync(lambda e: run("sp", e))


def build(stop_after=99, dbg=False, skip0=False):
    nc = bass.Bass("TRN2", target_bir_lowering=False)
    dk = "ExternalOutput" if dbg else "Internal"

    def din(name, shape, dt=F32):
        return nc.dram_tensor(name, shape, dt, kind="ExternalInput").ap()

    xh = din("xh", [NTOK, D])
    w_in = din("w_in", [D, 4096])
    pool_w = din("pool_w", [4, 256, 256])
    if not skip0:
        w_out = din("w_out", [D, D])
        w_up = din("w_up", [D, DFF])
        w_down = din("w_down", [DFF, D])
    vecs = din("vecs", [128, 64])
    gfin_d = din("gfin", [128, D])
    tab_d = din("tab", [128, 8, 3, 2, 128])
    tabf_d = din("tabf", [128, 8, 3, 128])
    invc_d = din("invc", [128, 4, T])
    idf_d = din("idf", [128, 128])
    out = nc.dram_tensor("out", [NOWN, D], F32, kind="ExternalOutput").ap()

    wup_s = nc.dram_tensor("wup_s", [16, 128, 16, 512], BF16).ap()
    wdn_s = nc.dram_tensor("wdn_s", [4, 8, 128, 8, 512], BF16).ap()
    wout_s = nc.dram_tensor("wout_s", [4, 128, 16, 512], BF16).ap()
    qT_s = nc.dram_tensor("qT_s", [8, 128, NOWN], BF16, kind=dk).ap()
    kT_s = nc.dram_tensor("kT_s", [8, 128, NTOK], BF16, kind=dk).ap()
    v_s = nc.dram_tensor("v_s", [NTOK, 1024], BF16, kind=dk).ap()
    aT_s = nc.dram_tensor("aT_s", [8, 128, NOWN], BF16, kind=dk).ap()
    bT_s = nc.dram_tensor("bT_s", [8, 128, NOWN], BF16, kind=dk).ap()

    with ExitStack() as stack:
        P = Prog(nc, stack)
        ps = [stack.enter_context(nc.psum_tensor("ps%d" % i, [128, 512], F32)) for i in range(8)]
        RP = [Res("ps%d" % i) for i in range(8)]
        bank = [0]

        def nb():
            b = bank[0]
            bank[0] = (b + 1) % 8
            return b

        off = [SB_BASE]
        ntens = [0]

        def sb(shape, dt, name="t"):
            nbytes = int(np.prod(shape[1:])) * (4 if dt == F32 else 2)
            ntens[0] += 1
            t = nc.alloc_sbuf_tensor_at("%s_%d" % (name, ntens[0]), shape, dt, offset=off[0])
            off[0] += (nbytes + 63) // 64 * 64
            assert off[0] <= SB_LIMIT, (name, off[0])
            return t

        idf_s = sb([128, 128], F32, "idf")
        idb = sb([128, 128], BF16, "idb")
        onesb = sb([128, 128], BF16, "ones")
        vec_s = sb([128, 64], F32, "vecs")
        ssq = sb([128, 64], F32, "ssq")
        junk = sb([128, D], BF16, "junk")
        R_id = Res()
        R_vec = Res()
        R_junk = Res()
        R_ssq = [Res() for _ in range(16)]
        R_ssqc = [Res() for _ in range(64)]
        ssq_ctr = [0]
        P.dma("sp", idf_s[:], idf_d, w=[R_id], sem="c0")
        P.dma("sp", vec_s[:], vecs, w=[R_vec], sem="c0")
        P.op("dve", lambda e: e.tensor_copy(out=idb[:], in_=idf_s[:]), r=[R_id], w=[R_id])
        P.op("dve", lambda e: e.memset(onesb[:], 1.0), w=[R_id])
        base_persist = off[0]

        def rstd_chain(grp, ncol, scale_div):
            c0 = grp * 4
            R = R_ssq[grp]
            P.op("dve", lambda e: e.tensor_scalar(out=ssq[:, c0:c0 + ncol], in0=ssq[:, c0:c0 + ncol], scalar1=1.0 / scale_div,
                                                  scalar2=EPS, op0=ALU.mult, op1=ALU.add), r=[R], w=[R])
            P.op("act", lambda e: e.activation(out=ssq[:, c0:c0 + ncol], in_=ssq[:, c0:c0 + ncol], func=AF.Sqrt), r=[R], w=[R])
            P.op("dve", lambda e: e.reciprocal(out=ssq[:, c0:c0 + ncol], in_=ssq[:, c0:c0 + ncol]), r=[R], w=[R])

        def rstd_chain_col(col, scale_div):
            R = R_ssqc[col]
            P.op("dve", lambda e: e.tensor_scalar(out=ssq[:, col:col + 1], in0=ssq[:, col:col + 1], scalar1=1.0 / scale_div,
                                                  scalar2=EPS, op0=ALU.mult, op1=ALU.add), r=[R], w=[R])
            P.op("act", lambda e: e.activation(out=ssq[:, col:col + 1], in_=ssq[:, col:col + 1], func=AF.Sqrt), r=[R], w=[R])
            P.op("dve", lambda e: e.reciprocal(out=ssq[:, col:col + 1], in_=ssq[:, col:col + 1]), r=[R], w=[R])

        def new_ssq_grp():
            g = ssq_ctr[0] % 16
            ssq_ctr[0] += 1
            c0 = g * 4
            P.op("dve", lambda e: e.memset(ssq[:, c0:c0 + 4], 0.0), w=[R_ssq[g]])
            return g

        evac_tog = [0]

        def evac_copy(out_ap, in_ap, r, w, scale=None):
            evac_tog[0] ^= 1
            if evac_tog[0]:
                if scale is None:
                    P.op("act", lambda e: e.activation(out=out_ap, in_=in_ap, func=AF.Copy), r=r, w=w)
                else:
                    P.op("act", lambda e: e.activation(out=out_ap, in_=in_ap, func=AF.Copy, scale=scale), r=r, w=w)
            else:
                if scale is None:
                    P.op("dve", lambda e: e.tensor_copy(out=out_ap, in_=in_ap), r=r, w=w)
                else:
                    P.op("dve", lambda e: e.tensor_scalar(out=out_ap, in0=in_ap, scalar1=scale, scalar2=None, op0=ALU.mult), r=r, w=w)

        off[0] = base_persist
        NST = 3
        stg_f = [sb([128, D], F32, "stgf") for _ in range(NST)]
        stg_b = [sb([128, D], BF16, "stgb") for _ in range(NST)]
        R_sf = [Res() for _ in range(NST)]
        R_sb = [Res() for _ in range(NST)]
        R_wup, R_wdn, R_wout = Res(), Res(), Res()
        jobs = []
        for kc in range(0 if skip0 else 16):
            for qd in range(4):
                jobs.append((w_up[kc * 128:(kc + 1) * 128, qd * 2048:(qd + 1) * 2048],
                             wup_s[qd * 4:(qd + 1) * 4, :, kc, :].rearrange("i p c -> p i c"), 16 + kc, R_wup))
        for f in range(0 if skip0 else 64):
            jobs.append((w_down[f * 128:(f + 1) * 128, :], wdn_s[:, f // 8, :, f % 8, :].rearrange("cb p c -> p cb c"), None, R_wdn))
        for kc in range(0 if skip0 else 16):
            jobs.append((w_out[kc * 128:(kc + 1) * 128, :], wout_s[:, :, kc, :].rearrange("cb p c -> p cb c"), 32 + kc, R_wout))
        cast_engs = ("dve", "act")

        def p0_load(i):
            src, dst, gcol, Rw = jobs[i]
            P.dma("sp", stg_f[i % NST][:], src, w=[R_sf[i % NST]], sem="sf%d" % (i % NST))

        def p0_cast(i):
            src, dst, gcol, Rw = jobs[i]
            k = i % NST
            eng = cast_engs[i % 2]
            o_ap = stg_b[k][:]
            i_ap = stg_f[k][:]
            if gcol is not None:
                sc = vec_s[:, gcol:gcol + 1]
                if eng == "act":
                    P.op("act", lambda e: e.activation(out=o_ap, in_=i_ap, func=AF.Copy, scale=sc), r=[R_sf[k], R_vec], w=[R_sb[k]])
                else:
                    P.op(eng, lambda e: e.tensor_scalar(out=o_ap, in0=i_ap, scalar1=sc, scalar2=None, op0=ALU.mult), r=[R_sf[k], R_vec], w=[R_sb[k]])
            else:
                if eng == "act":
                    P.op("act", lambda e: e.activation(out=o_ap, in_=i_ap, func=AF.Copy), r=[R_sf[k]], w=[R_sb[k]])
                else:
                    P.op(eng, lambda e: e.tensor_copy(out=o_ap, in_=i_ap), r=[R_sf[k]], w=[R_sb[k]])
            P.dma("sp", dst, stg_b[k][:].rearrange("p (i c) -> p i c", c=512), r=[R_sb[k]], w=[Rw], sem="sb%d" % k)

        if skip0:
            jobs = []
        for i in range(min(NST, len(jobs))):
            p0_load(i)
        for i in range(len(jobs)):
            p0_cast(i)
            if i + NST < len(jobs):
                p0_load(i + NST)
        P.barrier()
        if stop_after <= 0:
            P.barrier(["sp"])
            P.emit()
            return nc

        off[0] = base_persist
        Wres = sb([128, 16, 2048], BF16, "wres")
        R_W = Res()
        wst = [sb([128, 2048], F32, "wst") for _ in range(2)]
        R_wst = [Res(), Res()]
        NXR = 3
        xt = [sb([128, D], F32, "xt") for _ in range(NXR)]
        R_xt = [Res() for _ in range(NXR)]
        NHR = 3
        hb = [sb([128, D], BF16, "hb") for _ in range(NHR)]
        R_hb = [Res() for _ in range(NHR)]
        hT = [sb([128, 16, T], BF16, "hT")]
        R_hT = [Res(), Res()]
        base_p1 = off[0]
        hT.append(sb([128, 16, T], BF16, "hT"))

        def load_wres(col0, qscale_cols=None):
            for kc in range(16):
                k = kc % 2
                P.dma("sp", wst[k][:], w_in[kc * 128:(kc + 1) * 128, col0:col0 + 2048], w=[R_wst[k]], sem="wst%d" % k)
                sc = vec_s[:, kc:kc + 1]
                if qscale_cols is None:
                    if kc % 2:
                        P.op("dve", lambda e, kc=kc, k=k, sc=sc: e.tensor_scalar(
                            out=Wres[:, kc, :], in0=wst[k][:], scalar1=sc, scalar2=None, op0=ALU.mult), r=[R_wst[k], R_vec], w=[R_W])
                    else:
                        P.op("act", lambda e, kc=kc, k=k, sc=sc: e.activation(
                            out=Wres[:, kc, :], in_=wst[k][:], func=AF.Copy, scale=sc), r=[R_wst[k], R_vec], w=[R_W])
                else:
                    a, b = qscale_cols
                    P.op("act", lambda e, kc=kc, k=k, sc=sc: e.activation(
                        out=Wres[:, kc, 0:a], in_=wst[k][:, 0:a], func=AF.Copy, scale=sc), r=[R_wst[k], R_vec], w=[R_W])
                    P.op("dve", lambda e, kc=kc, k=k, sc=sc: e.tensor_scalar(
                        out=Wres[:, kc, a:b], in0=wst[k][:, a:b], scalar1=sc, scalar2=QSCALE, op0=ALU.mult, op1=ALU.mult), r=[R_wst[k], R_vec], w=[R_W])

        hctr = [0]
        xjobs = []
        xiss = [0]
        xcons = [0]

        def xload_ensure(n):
            while xiss[0] < min(n, len(xjobs)):
                i = xiss[0]
                k = i % NXR
                r0 = xjobs[i]
                P.dma("sp", xt[k][:], xh[r0:r0 + 128, :], w=[R_xt[k]], sem="xt%d" % k)
                xiss[0] += 1

        def front(j, hTi):
            g = new_ssq_grp()
            for s in range(4):
                i = xcons[0]
                xcons[0] += 1
                assert xjobs[i] == j * T + s * 128
                xload_ensure(i + NXR)
                k = i % NXR
                col = g * 4 + s
                Rc = R_ssqc[col]
                P.op("act", lambda e, k=k, col=col: e.activation(out=junk[:], in_=xt[k][:], func=AF.Square, accum_out=ssq[:, col:col + 1]),
                     r=[R_xt[k], R_ssq[g]], w=[R_junk, Rc])
                rstd_chain_col(col, D)
                hk = hctr[0] % NHR
                hctr[0] += 1
                if s % 2 == 0:
                    P.op("dve", lambda e, k=k, col=col, hk=hk: e.tensor_scalar(
                        out=hb[hk][:], in0=xt[k][:], scalar1=ssq[:, col:col + 1], scalar2=None, op0=ALU.mult),
                        r=[R_xt[k], Rc], w=[R_hb[hk]])
                else:
                    P.op("act", lambda e, k=k, col=col, hk=hk: e.activation(
                        out=hb[hk][:], in_=xt[k][:], func=AF.Copy, scale=ssq[:, col:col + 1]),
                        r=[R_xt[k], Rc], w=[R_hb[hk]])
                for half in range(2):
                    b = nb()
                    psb = ps[b][:].bitcast(BF16)
                    for kk in range(8):
                        kc = half * 8 + kk
                        P.op("pe", lambda e, psb=psb, kk=kk, kc=kc, hk=hk: e.transpose(
                            out=psb[:, kk * 128:(kk + 1) * 128], in_=hb[hk][:, kc * 128:(kc + 1) * 128], identity=idb[:]),
                            r=[R_hb[hk], R_id], w=[RP[b]])
                    evac_copy(hT[hTi][:, half * 8:(half + 1) * 8, s * 128:(s + 1) * 128],
                              psb[:, 0:1024].rearrange("p (k t) -> p k t", t=128), r=[RP[b]], w=[R_hT[hTi]])

        kst = [sb([128, 8, T], BF16, "kst") for _ in range(2)]
        vst = [sb([128, 4, 1024], BF16, "vst") for _ in range(2)]
        R_kst = [Res(), Res()]
        R_vst = [Res(), Res()]
        R_kT, R_v, R_qT, R_aT, R_bT = Res(), Res(), Res(), Res(), Res()
        load_wres(2048)
        xjobs.extend(j * T + s * 128 for j in range(NT_ALL) for s in range(4))
        for j in range(NT_ALL):
            hi = j % 2
            front(j, hi)
            k2 = j % 2
            for hd in range(8):
                b = nb()
                for kc in range(16):
                    P.op("pe", lambda e, b=b, kc=kc, hd=hd, hi=hi: e.matmul(ps[b][:], lhsT=Wres[:, kc, hd * 128:(hd + 1) * 128], rhs=hT[hi][:, kc, :],
                                                                         start=(kc == 0), stop=(kc == 15)), r=[R_W, R_hT[hi]], w=[RP[b]])
                evac_copy(kst[k2][:, hd, :], ps[b][:], r=[RP[b]], w=[R_kst[k2]])
            P.dma("sp", kT_s[:, :, j * T:(j + 1) * T].rearrange("h p t -> p h t"), kst[k2][:], r=[R_kst[k2]], w=[R_kT], sem="kst%d" % k2)
            for s in range(4):
                for half in range(2):
                    b = nb()
                    for kc in range(16):
                        P.op("pe", lambda e, b=b, kc=kc, s=s, half=half, hi=hi: e.matmul(
                            ps[b][:], lhsT=hT[hi][:, kc, s * 128:(s + 1) * 128], rhs=Wres[:, kc, 1024 + half * 512:1024 + (half + 1) * 512],
                            start=(kc == 0), stop=(kc == 15)), r=[R_W, R_hT[hi]], w=[RP[b]])
                    evac_copy(vst[k2][:, s, half * 512:(half + 1) * 512], ps[b][:], r=[RP[b]], w=[R_vst[k2]])
            P.dma("sp", v_s[j * T:(j + 1) * T, :].rearrange("(s p) c -> p s c", p=128), vst[k2][:], r=[R_vst[k2]], w=[R_v], sem="vst%d" % k2)
        P.barrier()
        if stop_after <= 1:
            P.barrier(["sp"])
            P.emit()
            return nc

        off[0] = base_p1
        U = sb([128, 8, 528], F32, "U")
        SA = sb([128, 2, 528], F32, "SA")
        SBb = sb([128, 2, 528], F32, "SB")
        dT = sb([128, 8, T], BF16, "dT")
        qst = sb([128, 8, T], BF16, "qst")
        ast = sb([128, 8, T], BF16, "ast")
        invc = sb([128, 4, T], F32, "invc")
        pwf = wst[0][:].rearrange("p (g c f) -> p g c f", g=4, c=2)
        pwb = sb([128, 4, 2, 256], BF16, "pwb")
        R_U, R_SA, R_SB, R_dT, R_qst, R_ast, R_invc, R_pw = Res(), Res(), Res(), Res(), Res(), Res(), Res(), Res()
        P.dma("sp", invc[:], invc_d, w=[R_invc], sem="c1")
        P.dma("sp", pwf, pool_w.rearrange("g (cc p) f -> p g cc f", p=128), w=[R_wst[0]], sem="wst0")
        P.op("dve", lambda e: e.tensor_copy(out=pwb[:], in_=pwf), r=[R_wst[0]], w=[R_pw])
        load_wres(0, qscale_cols=(1024, 2048))

        def y_mm(j):
            for g in range(4):
                for fc in range(2):
                    b = nb()
                    for cc in range(2):
                        P.op("pe", lambda e, b=b, g=g, fc=fc, cc=cc: e.matmul(ps[b][:], lhsT=pwb[:, g, cc, fc * 128:(fc + 1) * 128], rhs=dT[:, 2 * g + cc, :],
                                                                             start=(cc == 0), stop=(cc == 1)), r=[R_pw, R_dT], w=[RP[b]])
                    ch = 2 * g + fc
                    P.op("act", lambda e, b=b, ch=ch: e.activation(out=ast[:, ch, :], in_=ps[b][:], func=AF.Copy, scale=vec_s[:, 48 + ch:49 + ch]),
                         r=[RP[b], R_vec], w=[R_ast])
            P.dma("sp", aT_s[:, :, (j - NT_H) * T:(j - NT_H + 1) * T].rearrange("c p t -> p c t"), ast[:], r=[R_ast], w=[R_aT], sem="ast")

        xjobs.extend(j * T + s * 128 for j in range(NT_H - 1, NT_ALL) for s in range(4))
        for j in range(NT_H - 1, NT_ALL):
            front(j, 0)
            if j >= NT_H:
                for hd in range(8):
                    b = nb()
                    for kc in range(16):
                        P.op("pe", lambda e, b=b, kc=kc, hd=hd: e.matmul(ps[b][:], lhsT=Wres[:, kc, 1024 + hd * 128:1024 + (hd + 1) * 128], rhs=hT[0][:, kc, :],
                                                                     start=(kc == 0), stop=(kc == 15)), r=[R_W, R_hT[0]], w=[RP[b]])
                    evac_copy(qst[:, hd, :], ps[b][:], r=[RP[b]], w=[R_qst])
                P.dma("sp", qT_s[:, :, (j - NT_H) * T:(j - NT_H + 1) * T].rearrange("h p t -> p h t"), qst[:], r=[R_qst], w=[R_qT], sem="qst")
            if j > NT_H:
                y_mm(j - 1)
            for c in range(8):
                b = nb()
                for kc in range(16):
                    P.op("pe", lambda e, b=b, kc=kc, c=c: e.matmul(ps[b][:], lhsT=Wres[:, kc, c * 128:(c + 1) * 128], rhs=hT[0][:, kc, :],
                                                               start=(kc == 0), stop=(kc == 15)), r=[R_W, R_hT[0]], w=[RP[b]])
                evac_copy(U[:, c, 16:528], ps[b][:], r=[RP[b]], w=[R_U])
            if j >= NT_H:
                for g in range(4):
                    w_ = 2 ** (g + 1)
                    Ug = U[:, 2 * g:2 * g + 2, :]
                    P.op("pool", lambda e, Ug=Ug: e.tensor_tensor(out=SA[:, :, 2:528], in0=Ug[:, :, 2:528], in1=Ug[:, :, 1:527], op=ALU.add), r=[R_U], w=[R_SA])
                    Tr, R_T = SA, R_SA
                    if g >= 1:
                        P.op("pool", lambda e: e.tensor_tensor(out=SBb[:, :, 4:528], in0=SA[:, :, 4:528], in1=SA[:, :, 2:526], op=ALU.add), r=[R_SA], w=[R_SB])
                        Tr, R_T = SBb, R_SB
                    if g >= 2:
                        P.op("pool", lambda e: e.tensor_tensor(out=SA[:, :, 8:528], in0=SBb[:, :, 8:528], in1=SBb[:, :, 4:524], op=ALU.add), r=[R_SB], w=[R_SA])
                        Tr, R_T = SA, R_SA
                    if g >= 3:
                        P.op("pool", lambda e: e.tensor_tensor(out=SBb[:, :, 16:528], in0=SA[:, :, 16:528], in1=SA[:, :, 8:520], op=ALU.add), r=[R_SA], w=[R_SB])
                        Tr, R_T = SBb, R_SB
                    if j == NT_H:
                        for cc in range(2):
                            P.op("pool", lambda e, Tr=Tr, cc=cc, g=g: e.tensor_tensor(out=Tr[:, cc, 16:528], in0=Tr[:, cc, 16:528], in1=invc[:, g, :], op=ALU.mult),
                                 r=[R_T, R_invc], w=[R_T])
                        P.op("pool", lambda e, Tr=Tr, Ug=Ug, g=g: e.tensor_tensor(out=dT[:, 2 * g:2 * g + 2, :], in0=Tr[:, :, 16:528], in1=Ug[:, :, 16:528], op=ALU.subtract),
                             r=[R_T, R_U], w=[R_dT])
                    else:
                        P.op("dve", lambda e, Tr=Tr, Ug=Ug, g=g, w_=w_: e.scalar_tensor_tensor(out=dT[:, 2 * g:2 * g + 2, :], in0=Tr[:, :, 16:528], scalar=1.0 / w_,
                                                                                         in1=Ug[:, :, 16:528], op0=ALU.mult, op1=ALU.subtract),
                             r=[R_T, R_U], w=[R_dT])
            P.op("pool", lambda e: e.tensor_copy(out=U[:, :, 0:16], in_=U[:, :, 512:528]), r=[R_U], w=[R_U])
        y_mm(NT_ALL - 1)
        P.barrier()
        if stop_after <= 2:
            P.barrier(["sp"])
            P.emit()
            return nc

        off[0] = base_persist
        QT = [sb([128, NOWN], BF16, "QT") for _ in range(2)]
        KT = [sb([128, NTOK], BF16, "KT") for _ in range(2)]
        V3 = [sb([128, 3, 48, 128], BF16, "V3") for _ in range(2)]
        tabs = [sb([128, 3, 2, 128], F32, "tab") for _ in range(2)]
        tabf = [sb([128, 3, 128], F32, "tabf") for _ in range(2)]
        ND = sb([128, 2, NOWN], F32, "ND")
        BTs = [sb([128, NOWN], BF16, "BTs") for _ in range(2)]
        NSR = 3
        Sb = [sb([128, 512], F32, "Sb") for _ in range(NSR)]
        PT = [sb([128, 512], BF16, "PT") for _ in range(NSR)]
        R_QT, R_KT, R_V3, R_tab = [Res(), Res()], [Res(), Res()], [Res(), Res()], [Res(), Res()]
        R_ND = Res()
        R_BTs = [Res(), Res()]
        R_Sb = [Res() for _ in range(NSR)]
        R_PT = [Res() for _ in range(NSR)]

        def p2_load(hd):
            pr = hd % 2
            P.dma("sp", QT[pr][:], qT_s[hd], r=[R_qT], w=[R_QT[pr]], sem="QT%d" % pr)
            P.dma("sp", KT[pr][:], kT_s[hd], r=[R_kT], w=[R_KT[pr]], sem="KT%d" % pr)
            P.dma("sp", tabs[pr][:], tab_d[:, hd], w=[R_tab[pr]], sem="tab%d" % pr)
            P.dma("sp", tabf[pr][:], tabf_d[:, hd], w=[R_tab[pr]], sem="tab%d" % pr)
            for p, d in enumerate(PATS):
                M = 48 // d
                vv = v_s.rearrange("(m i r) c -> r m i c", r=d, i=128)
                if d == 16:
                    for r16 in range(16):
                        src = vv[r16, :, :, hd * 128:(hd + 1) * 128].rearrange("m i c -> i m c")
                        dst = V3[pr][:, p, r16 * 3:(r16 + 1) * 3, :]
                        P.dma("sp", dst, src, r=[R_v], w=[R_V3[pr]], sem="V3%d" % pr)
                    continue
                for q4 in range(4):
                    if d == 1:
                        src = vv[0, q4 * 12:(q4 + 1) * 12, :, hd * 128:(hd + 1) * 128].rearrange("m i c -> i m c")
                        dst = V3[pr][:, p, q4 * 12:(q4 + 1) * 12, :]
                    elif d == 4:
                        src = vv[q4, :, :, hd * 128:(hd + 1) * 128].rearrange("m i c -> i m c")
                        dst = V3[pr][:, p, q4 * 12:(q4 + 1) * 12, :]
                    else:
                        src = vv[q4 * 4:(q4 + 1) * 4, :, :, hd * 128:(hd + 1) * 128].rearrange("r m i c -> i r m c")
                        dst = V3[pr][:, p, q4 * 12:(q4 + 1) * 12, :].rearrange("i (r m) c -> i r m c", m=3)
                    P.dma("sp", dst, src, r=[R_v], w=[R_V3[pr]], sem="V3%d" % pr)

        units = []
        for hd in range(8):
            for p, d in enumerate(PATS):
                for r_ in range(d):
                    for mm0 in range(0, 32 // d, 2):
                        units.append((hd, p, d, r_, mm0))

        def stage1(n):
            hd, p, d, r_, mm0 = units[n]
            pr = hd % 2
            m0 = (HALO // 128) // d
            KTv = KT[pr][:].rearrange("p (m i r) -> p r m i", r=d, i=128)
            QTv = QT[pr][:].rearrange("p (m i r) -> p r m i", r=d, i=128)
            sr = n % NSR
            bs = nb()
            for qb in range(2):
                m = m0 + mm0 + qb
                for c in range(2):
                    P.op("pe", lambda e, bs=bs, qb=qb, c=c, m=m, r_=r_, mm0=mm0, KTv=KTv, QTv=QTv: e.matmul(
                        ps[bs][:, (qb * 2 + c) * 128:(qb * 2 + c + 1) * 128], lhsT=KTv[:, r_, m - 1 + c, :], rhs=QTv[:, r_, mm0 + qb, :],
                        start=True, stop=True), r=[R_KT[pr], R_QT[pr]], w=[RP[bs]])
            for qb in range(2):
                if mm0 + qb == 0:
                    P.op("dve", lambda e, bs=bs, qb=qb, sr=sr, p=p, pr=pr: e.tensor_tensor(
                        out=Sb[sr][:, qb * 256:qb * 256 + 128], in0=ps[bs][:, qb * 256:qb * 256 + 128], in1=tabf[pr][:, p, :], op=ALU.add),
                        r=[RP[bs], R_tab[pr]], w=[R_Sb[sr]])
                    P.op("dve", lambda e, bs=bs, qb=qb, sr=sr, p=p, pr=pr: e.tensor_tensor(
                        out=Sb[sr][:, qb * 256 + 128:qb * 256 + 256], in0=ps[bs][:, qb * 256 + 128:qb * 256 + 256], in1=tabs[pr][:, p, 1, :], op=ALU.add),
                        r=[RP[bs], R_tab[pr]], w=[R_Sb[sr]])
                else:
                    P.op("dve", lambda e, bs=bs, qb=qb, sr=sr, p=p, pr=pr: e.tensor_tensor(
                        out=Sb[sr][:, qb * 256:(qb + 1) * 256], in0=ps[bs][:, qb * 256:(qb + 1) * 256],
                        in1=tabs[pr][:, p, :, :].rearrange("k c q -> k (c q)"), op=ALU.add),
                        r=[RP[bs], R_tab[pr]], w=[R_Sb[sr]])
            P.op("act", lambda e, sr=sr: e.activation(out=PT[sr][:], in_=Sb[sr][:], func=AF.Exp), r=[R_Sb[sr]], w=[R_PT[sr]])

        def stage2(n):
            hd, p, d, r_, mm0 = units[n]
            pr = hd % 2
            M = 48 // d
            m0 = (HALO // 128) // d
            NDv = ND[:].rearrange("p n (m i r) -> p n r m i", r=d, i=128)
            sr = n % NSR
            bo = nb()
            for qb in range(2):
                m = m0 + mm0 + qb
                for c in range(2):
                    tau = r_ * M + m - 1 + c
                    P.op("pe", lambda e, bo=bo, qb=qb, c=c, tau=tau, sr=sr, p=p, pr=pr: e.matmul(
                        ps[bo][:, qb * 128:(qb + 1) * 128], lhsT=V3[pr][:, p, tau, :], rhs=PT[sr][:, (qb * 2 + c) * 128:(qb * 2 + c + 1) * 128],
                        start=(c == 0), stop=(c == 1)), r=[R_V3[pr], R_PT[sr]], w=[RP[bo]])
                for c in range(2):
                    P.op("pe", lambda e, bo=bo, qb=qb, c=c, sr=sr: e.matmul(
                        ps[bo][:, 256 + qb * 128:256 + (qb + 1) * 128], lhsT=onesb[:], rhs=PT[sr][:, (qb * 2 + c) * 128:(qb * 2 + c + 1) * 128],
                        start=(c == 0), stop=(c == 1)), r=[R_id, R_PT[sr]], w=[RP[bo]])
            for qb in range(2):
                tgt = NDv[:, :, r_, mm0 + qb, :]
                src = ps[bo][:].rearrange("p (n q i) -> p n q i", n=2, q=2)[:, :, qb, :]
                if p == 0:
                    P.op("act", lambda e, tgt=tgt, src=src: e.activation(out=tgt, in_=src, func=AF.Copy), r=[RP[bo]], w=[R_ND])
                else:
                    P.op("dve", lambda e, tgt=tgt, src=src: e.tensor_tensor(out=tgt, in0=src, in1=tgt, op=ALU.add), r=[RP[bo], R_ND], w=[R_ND])

        def head_finish(hd):
            pr = hd % 2
            for q4 in range(4):
                sl = slice(q4 * 1024, (q4 + 1) * 1024)
                P.op("dve", lambda e, sl=sl: e.reciprocal(out=ND[:, 1, sl], in_=ND[:, 1, sl]), r=[R_ND], w=[R_ND])
                P.op("pool", lambda e, sl=sl, pr=pr: e.tensor_tensor(out=BTs[pr][:, sl], in0=ND[:, 0, sl], in1=ND[:, 1, sl], op=ALU.mult), r=[R_ND], w=[R_BTs[pr]])
            P.dma("sp", bT_s[hd], BTs[pr][:], r=[R_BTs[pr]], w=[R_bT], sem="BTs%d" % pr)

        p2_load(0)
        p2_load(1)
        NU = len(units)
        stage1(0)
        for n in range(NU):
            if n + 1 < NU:
                stage1(n + 1)
            stage2(n)
            hd = units[n][0]
            if n + 1 == NU or units[n + 1][0] != hd:
                head_finish(hd)
                if hd + 2 < 8:
                    p2_load(hd + 2)
        P.barrier()
        if stop_after <= 3:
            P.barrier(["sp"])
            P.emit()
            return nc

        off[0] = base_persist
        xr = [sb([128, D], F32, "xr") for _ in range(4)]
        R_xr = [Res() for _ in range(4)]
        yT = sb([128, 16, T], BF16, "yT")
        R_yT = Res()
        ysq = [sb([128, T], BF16, "ysq") for _ in range(3)]
        R_ysq = [Res() for _ in range(3)]
        gfin = sb([128, D], F32, "gfin")
        R_gfin = Res()
        h2 = [sb([128, D], BF16, "h2") for _ in range(2)]
        R_h2 = [Res(), Res()]
        h2T = sb([128, 16, T], BF16, "h2T")
        R_h2T = Res()
        aT = sb([128, 64, T], BF16, "aT")
        R_aTc = [Res() for _ in range(64)]
        wring = [sb([128, 16, 512], BF16, "wring") for _ in range(2)]
        R_wr = [Res(), Res()]
        wdr = [sb([128, 8, 512], BF16, "wdr") for _ in range(2)]
        R_wd = [Res(), Res()]
        rpa = sb([128, 8], F32, "rpa")
        R_rpa = Res()
        P.dma("sp", gfin[:], gfin_d, w=[R_gfin], sem="c1")

        wr_list = []
        wd_list = []
        for j in range(NT_OWN):
            for cb in range(4):
                wr_list.append((wout_s[cb], R_wout))
            for i in range(16):
                wr_list.append((wup_s[i], R_wup))
            for cb in range(4):
                for fg in range(8):
                    wd_list.append((wdn_s[cb, fg], R_wdn))
        wr_iss = [0]
        wd_iss = [0]

        def wr_ensure(n):
            while wr_iss[0] < min(n, len(wr_list)):
                i = wr_iss[0]
                src, Rs = wr_list[i]
                P.dma("sp", wring[i % 2][:], src, r=[Rs], w=[R_wr[i % 2]], sem="wr%d" % (i % 2))
                wr_iss[0] += 1

        def wd_ensure(n):
            while wd_iss[0] < min(n, len(wd_list)):
                i = wd_iss[0]
                src, Rs = wd_list[i]
                P.dma("sp", wdr[i % 2][:], src, r=[Rs], w=[R_wd[i % 2]], sem="wd%d" % (i % 2))
                wd_iss[0] += 1

        wr_c = [0]
        wd_c = [0]
        yq = [0]
        h2c = [0]

        def p3_loads(j):
            P.dma("sp", yT[:, 0:8, :], aT_s[:, :, j * T:(j + 1) * T].rearrange("c p t -> p c t"), r=[R_aT], w=[R_yT], sem="yT")
            P.dma("sp", yT[:, 8:16, :], bT_s[:, :, j * T:(j + 1) * T].rearrange("c p t -> p c t"), r=[R_bT], w=[R_yT], sem="yT")
            for s in range(4):
                r0 = HALO + j * T + s * 128
                P.dma("sp", xr[s][:], xh[r0:r0 + 128, :], w=[R_xr[s]], sem="xr%d" % s)

        for j in range(NT_OWN):
            wr_ensure(wr_c[0] + 1)
            p3_loads(j)
            bq = nb()
            P.op("dve", lambda e, bq=bq: e.memset(ps[bq][:, 0:8], 0.0), w=[RP[bq]])
            for kc in range(16):
                yk = yq[0] % 3
                yq[0] += 1
                P.op("pool", lambda e, kc=kc, yk=yk: e.tensor_tensor(out=ysq[yk][:], in0=yT[:, kc, :], in1=yT[:, kc, :], op=ALU.mult), r=[R_yT], w=[R_ysq[yk]])
                for s in range(4):
                    col = (kc // 8) * 4 + s
                    P.op("pe", lambda e, bq=bq, yk=yk, s=s, col=col: e.matmul(ps[bq][:, col:col + 1], lhsT=ysq[yk][:, s * 128:(s + 1) * 128], rhs=onesb[:, 0:1],
                                                                            start=False, stop=False, skip_group_check=True), r=[R_ysq[yk], R_id], w=[RP[bq]])
            P.op("dve", lambda e, bq=bq: e.tensor_scalar(out=rpa[:], in0=ps[bq][:, 0:8], scalar1=1.0 / 1024, scalar2=EPS, op0=ALU.mult, op1=ALU.add), r=[RP[bq]], w=[R_rpa])
            P.op("act", lambda e: e.activation(out=rpa[:], in_=rpa[:], func=AF.Sqrt), r=[R_rpa], w=[R_rpa])
            P.op("dve", lambda e: e.reciprocal(out=rpa[:], in_=rpa[:]), r=[R_rpa], w=[R_rpa])
            for cb in range(4):
                wi = wr_c[0]
                wr_c[0] += 1
                wr_ensure(wi + 2)
                wk = wi % 2
                for s in range(4):
                    bp, ba = nb(), nb()
                    for kc in range(16):
                        b = bp if kc < 8 else ba
                        P.op("pe", lambda e, b=b, kc=kc, s=s, wk=wk: e.matmul(ps[b][:], lhsT=yT[:, kc, s * 128:(s + 1) * 128], rhs=wring[wk][:, kc, :],
                                                                            start=(kc % 8 == 0), stop=(kc % 8 == 7)), r=[R_yT, R_wr[wk]], w=[RP[b]])
                    xs = xr[s][:, cb * 512:(cb + 1) * 512]
                    P.op("dve", lambda e, bp=bp, s=s, xs=xs: e.scalar_tensor_tensor(out=xs, in0=ps[bp][:], scalar=rpa[:, s:s + 1], in1=xs, op0=ALU.mult, op1=ALU.add),
                         r=[RP[bp], R_rpa, R_xr[s]], w=[R_xr[s]])
                    P.op("dve", lambda e, ba=ba, s=s, xs=xs: e.scalar_tensor_tensor(out=xs, in0=ps[ba][:], scalar=rpa[:, 4 + s:5 + s], in1=xs, op0=ALU.mult, op1=ALU.add),
                         r=[RP[ba], R_rpa, R_xr[s]], w=[R_xr[s]])
            g = new_ssq_grp()
            for s in range(4):
                P.op("act", lambda e, s=s, g=g: e.activation(out=junk[:], in_=xr[s][:], func=AF.Square, accum_out=ssq[:, g * 4 + s:g * 4 + s + 1]),
                     r=[R_xr[s]], w=[R_junk, R_ssq[g]])
            rstd_chain(g, 4, D)
            for s in range(4):
                hk = h2c[0] % 2
                h2c[0] += 1
                if s % 2 == 0:
                    P.op("dve", lambda e, s=s, g=g, hk=hk: e.tensor_scalar(
                        out=h2[hk][:], in0=xr[s][:], scalar1=ssq[:, g * 4 + s:g * 4 + s + 1], scalar2=None, op0=ALU.mult), r=[R_xr[s], R_ssq[g]], w=[R_h2[hk]])
                else:
                    P.op("act", lambda e, s=s, g=g, hk=hk: e.activation(
                        out=h2[hk][:], in_=xr[s][:], func=AF.Copy, scale=ssq[:, g * 4 + s:g * 4 + s + 1]), r=[R_xr[s], R_ssq[g]], w=[R_h2[hk]])
                for half in range(2):
                    b = nb()
                    psb = ps[b][:].bitcast(BF16)
                    for kk in range(8):
                        kc = half * 8 + kk
                        P.op("pe", lambda e, psb=psb, kk=kk, kc=kc, hk=hk: e.transpose(out=psb[:, kk * 128:(kk + 1) * 128], in_=h2[hk][:, kc * 128:(kc + 1) * 128], identity=idb[:]),
                             r=[R_h2[hk], R_id], w=[RP[b]])
                    evac_copy(h2T[:, half * 8:(half + 1) * 8, s * 128:(s + 1) * 128], psb[:, 0:1024].rearrange("p (k t) -> p k t", t=128), r=[RP[b]], w=[R_h2T])
            for i in range(16):
                wi = wr_c[0]
                wr_c[0] += 1
                wr_ensure(wi + 2)
                if i == 12:
                    wd_ensure(wd_c[0] + 1)
                wk = wi % 2
                for fl in range(4):
                    f = i * 4 + fl
                    b = nb()
                    for kc in range(16):
                        P.op("pe", lambda e, b=b, kc=kc, fl=fl, wk=wk: e.matmul(ps[b][:], lhsT=wring[wk][:, kc, fl * 128:(fl + 1) * 128], rhs=h2T[:, kc, :],
                                                                             start=(kc == 0), stop=(kc == 15)), r=[R_wr[wk], R_h2T], w=[RP[b]])
                    P.op("act", lambda e, b=b, f=f: e.activation(out=aT[:, f, :], in_=ps[b][:], func=AF.Relu), r=[RP[b]], w=[R_aTc[f]])
                    P.op("pool", lambda e, f=f: e.tensor_tensor(out=aT[:, f, :], in0=aT[:, f, :], in1=aT[:, f, :], op=ALU.mult), r=[R_aTc[f]], w=[R_aTc[f]])
            for cb in range(4):
                bs4 = [nb() for _ in range(4)]
                for fg in range(8):
                    wi = wd_c[0]
                    wd_c[0] += 1
                    wd_ensure(wi + 2)
                    wk = wi % 2
                    for fl in range(8):
                        f = fg * 8 + fl
                        for s in range(4):
                            b = bs4[s]
                            P.op("pe", lambda e, b=b, f=f, fl=fl, s=s, wk=wk: e.matmul(ps[b][:], lhsT=aT[:, f, s * 128:(s + 1) * 128], rhs=wdr[wk][:, fl, :],
                                                                                     start=(f == 0), stop=(f == 63)), r=[R_aTc[f], R_wd[wk]], w=[RP[b]])
                for s in range(4):
                    xs = xr[s][:, cb * 512:(cb + 1) * 512]
                    b = bs4[s]
                    P.op("dve", lambda e, b=b, xs=xs: e.tensor_tensor(out=xs, in0=ps[b][:], in1=xs, op=ALU.add), r=[RP[b], R_xr[s]], w=[R_xr[s]])
            g = new_ssq_grp()
            for s in range(4):
                P.op("act", lambda e, s=s, g=g: e.activation(out=junk[:], in_=xr[s][:], func=AF.Square, accum_out=ssq[:, g * 4 + s:g * 4 + s + 1]),
                     r=[R_xr[s]], w=[R_junk, R_ssq[g]])
            rstd_chain(g, 4, D)
            for s in range(4):
                P.op("dve", lambda e, s=s, g=g: e.scalar_tensor_tensor(
                    out=xr[s][:], in0=xr[s][:], scalar=ssq[:, g * 4 + s:g * 4 + s + 1], in1=gfin[:], op0=ALU.mult, op1=ALU.mult),
                    r=[R_xr[s], R_ssq[g], R_gfin], w=[R_xr[s]])
                r0 = j * T + s * 128
                P.dma("pool", out[r0:r0 + 128, :], xr[s][:], r=[R_xr[s]], sem="out")
        P.barrier(["sp", "pool"])
        P.emit()
    return nc


_NC_CACHE = {}


def _host_tables(seq_start):
    slopes = 2.0 ** (-8.0 * np.arange(1, 9, dtype=np.float64) / 8)
    k = np.arange(128)[:, None]
    q = np.arange(128)[None, :]
    tab = np.zeros((128, 8, 3, 2, 128), np.float32)
    tabf = np.zeros((128, 8, 3, 128), np.float32)
    NEG = -30000.0
    for hd in range(8):
        for p, d in enumerate(PATS):
            j0 = q - k + 128
            t0 = np.where(k >= q, -slopes[hd] * d * j0, NEG)
            j1 = q - k
            t1 = np.where(k <= q, -slopes[hd] * d * j1, NEG)
            tab[:, hd, p, 0, :] = t0
            tab[:, hd, p, 1, :] = t1
            tabf[:, hd, p, :] = NEG if seq_start else t0
    invc = np.zeros((128, 4, T), np.float32)
    tpos = np.arange(T, dtype=np.float64)
    for g in range(4):
        w = 2 ** (g + 1)
        if seq_start:
            invc[:, g, :] = (1.0 / np.minimum(tpos + 1, w))[None, :]
        else:
            invc[:, g, :] = 1.0 / w
    return tab, tabf, invc


def kernel(x, norm_mix_g, w_in, pool_w, pool_scale, pool_out_norm_g, attn_out_norm_g,
           w_out, norm_mlp_g, w_up, w_down, norm_final_g):
    x = np.asarray(x, np.float32)
    B, S, _ = x.shape
    n = 8
    per = S // NOWN
    if "nc" not in _NC_CACHE:
        _NC_CACHE["nc"] = build()
    nc = _NC_CACHE["nc"]

    def col(v):
        return np.ascontiguousarray(np.asarray(v, np.float32).reshape(-1, 128).T)

    vecs = np.zeros((128, 64), np.float32)
    vecs[:, 0:16] = col(norm_mix_g[0])
    vecs[:, 16:32] = col(norm_mlp_g[0])
    vecs[:, 32:40] = col(pool_out_norm_g[0])
    vecs[:, 40:48] = col(attn_out_norm_g[0])
    vecs[:, 48:56] = col(pool_scale[0])
    gfin = np.ascontiguousarray(np.broadcast_to(np.asarray(norm_final_g, np.float32)[None, :], (128, D)))
    idf = np.eye(128, dtype=np.float32)
    shared = {
        "w_in": np.ascontiguousarray(np.asarray(w_in, np.float32)[0]),
        "pool_w": np.ascontiguousarray(np.asarray(pool_w, np.float32)[0]),
        "w_out": np.ascontiguousarray(np.asarray(w_out, np.float32)[0]),
        "w_up": np.ascontiguousarray(np.asarray(w_up, np.float32)[0]),
        "w_down": np.ascontiguousarray(np.asarray(w_down, np.float32)[0]),
        "vecs": vecs, "gfin": gfin, "idf": idf,
    }
    tabs = {True: _host_tables(True), False: _host_tables(False)}
    in_maps = []
    for c in range(n):
        b = c // per
        t0 = (c % per) * NOWN
        xh = np.zeros((NTOK, D), np.float32)
        if t0 == 0:
            xh[HALO:] = x[b, 0:NOWN]
        else:
            xh[:] = x[b, t0 - HALO:t0 + NOWN]
        tab, tabf, invc = tabs[t0 == 0]
        m = dict(shared)
        m.update({"xh": xh, "tab": tab, "tabf": tabf, "invc": invc})
        in_maps.append(m)
    res = run_bass_kernel_spmd(nc, in_maps, core_ids=list(range(n)))
    outp = np.empty((B, S, D), np.float32)
    for c in range(n):
        b = c // per
        t0 = (c % per) * NOWN
        outp[b, t0:t0 + NOWN] = res.results[c]["out"]
    return outp
```
